# Optimizing a Trainium2 kernel written in Bass

```python
import jax
import jax.numpy as jnp
from jax import lax
import numpy as np

D_MODEL = 1024
BATCH = 4
SEQ = 4096
DEPTH = 1
DEC_BATCH = 8
DEC_SEQ = 2048
PAST_LEN = 128

RWKV_HEADS = 8
RWKV_HEAD_DIM = 64
RWKV_WIDTH = RWKV_HEADS * RWKV_HEAD_DIM
DECAY_LORA = 64
ICLR_LORA = 64
GATE_LORA = 128
RET_HEADS = 4
RET_HEAD_DIM = 128
RET_WIDTH = RET_HEADS * RET_HEAD_DIM
RET_CHUNK = 128
ROPE_BASE = 10000.0
D_FF = 2816
RWKV_COLS = 3 * RWKV_WIDTH + DECAY_LORA + ICLR_LORA + GATE_LORA
RET_COLS = 4 * RET_WIDTH
IN_COLS = RWKV_COLS + RET_COLS + 2 * D_MODEL
ALPHA = (2.0 * DEPTH) ** 0.25
BETA = (8.0 * DEPTH) ** -0.25
LN_EPS = 1e-5
RWKV_GN_EPS = 64e-5
RET_GN_EPS = 1e-6
F32 = jnp.float32

kernel_name = 'hybrid_rwkv7_retention_macaron_encoder'


def layer_norm(x, g, b, eps=LN_EPS):
    xf = x.astype(F32)
    mu = jnp.mean(xf, axis=-1, keepdims=True)
    var = jnp.mean(jnp.square(xf - mu), axis=-1, keepdims=True)
    return ((xf - mu) * lax.rsqrt(var + eps) * g + b).astype(x.dtype)


def head_norm(y, eps):
    yf = y.astype(F32)
    mu = jnp.mean(yf, axis=-1, keepdims=True)
    var = jnp.mean(jnp.square(yf - mu), axis=-1, keepdims=True)
    return (yf - mu) * lax.rsqrt(var + eps)


def swiglu(x, w_up, w_down):
    gate, up = jnp.split(x @ w_up, 2, axis=-1)
    return (jax.nn.silu(gate) * up) @ w_down


def centred_shift(u, mu_prev, mu_next):
    zero = jnp.zeros_like(u[:, :1])
    prev = jnp.concatenate([zero, u[:, :-1]], axis=1)
    nxt = jnp.concatenate([u[:, 1:], zero], axis=1)
    return u + mu_prev * (prev - u) + mu_next * (nxt - u)


def both_dirs(t_fwd, t_bwd):
    s = jnp.stack([t_fwd, jnp.flip(t_bwd, axis=1)], axis=0)
    return jnp.moveaxis(s, 2, 0).astype(F32)


def rwkv7_scan(r, w, k, v, a, b):
    def step(S, inp):
        r_t, w_t, k_t, v_t, a_t, b_t = inp
        sa = jnp.einsum('dbhvk,dbhk->dbhv', S, a_t)
        S = S * w_t[..., None, :] + sa[..., :, None] * b_t[..., None, :] + v_t[..., :, None] * k_t[..., None, :]
        return S, jnp.einsum('dbhvk,dbhk->dbhv', S, r_t)
    S0 = jnp.zeros(r.shape[1:] + (r.shape[-1],), F32)
    _, y = lax.scan(step, S0, (r, w, k, v, a, b))
    return y


def rwkv7_branch(u, mu_prev, mu_next, w0, w2, a0, a2, g2, k_k, k_a, r_k, lnx_g, lnx_b):
    B, T, _ = u.shape
    H, N, W = RWKV_HEADS, RWKV_HEAD_DIM, RWKV_WIDTH
    u = centred_shift(u, mu_prev, mu_next)
    r, k, v, xw, xa, xg = jnp.split(
        u, [W, 2 * W, 3 * W, 3 * W + DECAY_LORA, 3 * W + DECAY_LORA + ICLR_LORA], axis=-1)
    heads = lambda t: t.reshape(t.shape[:-1] + (H, N))
    w_log = -jax.nn.softplus(-(w0[:, None, None, :] + jnp.einsum('btr,drc->dbtc', jnp.tanh(xw), w2))) - 0.5
    decay = heads(jnp.exp(-jnp.exp(w_log.astype(F32))))
    a = jax.nn.sigmoid(a0[:, None, None, :] + jnp.einsum('btr,drc->dbtc', xa, a2))
    g = jax.nn.sigmoid(xg) @ g2
    kk = heads((k * k_k).astype(F32))
    kk = kk / jnp.maximum(jnp.linalg.norm(kk, axis=-1, keepdims=True), 1e-12)
    k_dir = heads(k[None] * (1.0 + (a - 1.0) * k_a))
    b_dir = kk[None] * heads(a)
    r_h, v_h = heads(r), heads(v)
    y = rwkv7_scan(both_dirs(r_h, r_h), both_dirs(decay[0], decay[1]), both_dirs(k_dir[0], k_dir[1]),
                   both_dirs(v_h, v_h), both_dirs(-kk, -kk), both_dirs(b_dir[0], b_dir[1]))
    y = jnp.moveaxis(y, 0, 2)
    y = y[0] + jnp.flip(y[1], axis=1)
    y = head_norm(y, RWKV_GN_EPS) * heads(lnx_g) + heads(lnx_b)
    bonus = jnp.sum(jnp.sum(r_h[None] * k_dir * r_k, axis=-1, keepdims=True), axis=0)
    y = (y + bonus * v_h).reshape(B, T, W)
    return (y * g).astype(u.dtype)


def rotate_every_two(x):
    x1 = x[..., ::2]
    x2 = x[..., 1::2]
    return jnp.stack([-x2, x1], axis=-1).reshape(x.shape)


def retention_branch(u, pos):
    B, T, _ = u.shape
    H, D, C = RET_HEADS, RET_HEAD_DIM, RET_CHUNK
    NC = T // C
    q, k, v, g = jnp.split(u, 4, axis=-1)
    q, k, v = (t.reshape(B, T, H, D) for t in (q, k, v))
    angle = jnp.repeat(1.0 / (ROPE_BASE ** jnp.linspace(0.0, 1.0, D // 2, dtype=F32)), 2)
    theta = pos[:, None] * angle[None, :]
    sin, cos = jnp.sin(theta)[:, None, :], jnp.cos(theta)[:, None, :]
    q = q * cos + rotate_every_two(q) * sin
    k = (k * cos + rotate_every_two(k) * sin) * (D ** -0.5)
    log_g = jnp.log(1.0 - 2.0 ** (-5.0 - jnp.arange(H, dtype=F32)))
    idx = jnp.arange(C, dtype=F32)
    qc = q.reshape(B, NC, C, H, D)
    kc = k.reshape(B, NC, C, H, D)
    vc = v.reshape(B, NC, C, H, D)
    intra = jnp.exp(log_g[:, None, None] * jnp.abs(idx[:, None] - idx[None, :]))
    scores = jnp.einsum('bnihd,bnjhd->bnhij', qc, kc) * intra
    o = jnp.einsum('bnhij,bnjhd->bnihd', scores, vc)
    kv_f = jnp.einsum('bnjhd,hj,bnjhe->nbhde', kc, jnp.exp(log_g[:, None] * (C - 1.0 - idx)[None]), vc)
    kv_b = jnp.einsum('bnjhd,hj,bnjhe->nbhde', kc, jnp.exp(log_g[:, None] * idx[None]), vc)
    kv = jnp.stack([kv_f, jnp.flip(kv_b, axis=0)], axis=1).astype(F32)
    decay_chunk = jnp.exp(log_g * C)[:, None, None]

    def step(R, kv_n):
        return decay_chunk * R + kv_n, R

    _, states = lax.scan(step, jnp.zeros(kv.shape[1:], F32), kv)
    st_f = states[:, 0]
    st_b = jnp.flip(states[:, 1], axis=0)
    o = (o
         + jnp.einsum('bnihd,hi,nbhde->bnihe', qc, jnp.exp(log_g[:, None] * (idx + 1.0)[None]), st_f)
         + jnp.einsum('bnihd,hi,nbhde->bnihe', qc, jnp.exp(log_g[:, None] * (C - idx)[None]), st_b))
    o = head_norm(o.reshape(B, T, H, D), RET_GN_EPS).reshape(B, T, H * D)
    return (jax.nn.silu(g.astype(F32)) * o).astype(u.dtype)


def encoder_layer(x, ffn1_up, ffn1_down, ln1_g, ln1_b, w_in, shift_prev, shift_next,
                  rwkv_w0, rwkv_w2, rwkv_a0, rwkv_a2, rwkv_g2, rwkv_k_k, rwkv_k_a, rwkv_r_k,
                  rwkv_lnx_g, rwkv_lnx_b, w_branch_a, w_branch_b, w_out, ln2_g, ln2_b,
                  ffn2_up, ffn2_down, ln3_g, ln3_b):
    pos = jnp.arange(x.shape[1], dtype=F32)
    x = layer_norm(ALPHA * x + 0.5 * swiglu(x, ffn1_up, ffn1_down), ln1_g, ln1_b)
    u = x @ w_in
    u_a, u_b, gate_a, gate_b = jnp.split(
        u, [RWKV_COLS, RWKV_COLS + RET_COLS, RWKV_COLS + RET_COLS + D_MODEL], axis=-1)
    h_a = rwkv7_branch(u_a, shift_prev, shift_next, rwkv_w0, rwkv_w2, rwkv_a0, rwkv_a2, rwkv_g2,
                       rwkv_k_k, rwkv_k_a, rwkv_r_k, rwkv_lnx_g, rwkv_lnx_b) @ w_branch_a
    h_b = retention_branch(u_b, pos) @ w_branch_b
    mix = (jax.nn.sigmoid(gate_a) * h_a + jax.nn.sigmoid(gate_b) * h_b) @ w_out
    x = layer_norm(ALPHA * x + mix, ln2_g, ln2_b)
    x = layer_norm(ALPHA * x + 0.5 * swiglu(x, ffn2_up, ffn2_down), ln3_g, ln3_b)
    return x


def setup_inputs(seed: int = 0) -> dict:
    key = jax.random.key(seed)
    ks = iter(jax.random.split(key, 40))

    def nrm(shape, scale):
        return scale * jax.random.normal(next(ks), shape, F32)

    def uni(shape, lo, hi):
        return jax.random.uniform(next(ks), shape, F32, lo, hi)

    L, D, W = DEPTH, D_MODEL, RWKV_WIDTH
    ch = jnp.arange(W, dtype=F32) / (W - 1)
    w0_base = -6.0 + 5.0 * ch ** 0.9 + 0.5
    return {
        'x_prompt': nrm((BATCH, SEQ, D), 1.0),
        'x_sample': nrm((DEC_BATCH, DEC_SEQ, D), 1.0),
        'ffn1_up': nrm((L, D, 2 * D_FF), D ** -0.5),
        'ffn1_down': nrm((L, D_FF, D), BETA * D_FF ** -0.5),
        'ln1_g': 1.0 + nrm((L, D), 0.02),
        'ln1_b': nrm((L, D), 0.02),
        'w_in': nrm((L, D, IN_COLS), D ** -0.5),
        'shift_prev': uni((L, RWKV_COLS), 0.0, 0.5),
        'shift_next': uni((L, RWKV_COLS), 0.0, 0.5),
        'rwkv_w0': w0_base + nrm((L, 2, W), 0.1),
        'rwkv_w2': nrm((L, 2, DECAY_LORA, W), 0.1 * DECAY_LORA ** -0.5),
        'rwkv_a0': nrm((L, 2, W), 0.1),
        'rwkv_a2': nrm((L, 2, ICLR_LORA, W), 0.5 * ICLR_LORA ** -0.5),
        'rwkv_g2': nrm((L, GATE_LORA, W), GATE_LORA ** -0.5),
        'rwkv_k_k': 0.85 + nrm((L, W), 0.02),
        'rwkv_k_a': 1.0 + nrm((L, W), 0.02),
        'rwkv_r_k': nrm((L, RWKV_HEADS, RWKV_HEAD_DIM), 0.1),
        'rwkv_lnx_g': 1.0 + nrm((L, W), 0.02),
        'rwkv_lnx_b': nrm((L, W), 0.02),
        'w_branch_a': nrm((L, W, D), BETA * W ** -0.5),
        'w_branch_b': nrm((L, RET_WIDTH, D), BETA * RET_WIDTH ** -0.5),
        'w_out': nrm((L, D, D), BETA * D ** -0.5),
        'ln2_g': 1.0 + nrm((L, D), 0.02),
        'ln2_b': nrm((L, D), 0.02),
        'ffn2_up': nrm((L, D, 2 * D_FF), D ** -0.5),
        'ffn2_down': nrm((L, D_FF, D), BETA * D_FF ** -0.5),
        'ln3_g': 1.0 + nrm((L, D), 0.02),
        'ln3_b': nrm((L, D), 0.02),
    }


def reference(x_prompt, x_sample, ffn1_up, ffn1_down, ln1_g, ln1_b, w_in, shift_prev, shift_next,
              rwkv_w0, rwkv_w2, rwkv_a0, rwkv_a2, rwkv_g2, rwkv_k_k, rwkv_k_a, rwkv_r_k,
              rwkv_lnx_g, rwkv_lnx_b, w_branch_a, w_branch_b, w_out, ln2_g, ln2_b,
              ffn2_up, ffn2_down, ln3_g, ln3_b):
    params = (ffn1_up, ffn1_down, ln1_g, ln1_b, w_in, shift_prev, shift_next,
              rwkv_w0, rwkv_w2, rwkv_a0, rwkv_a2, rwkv_g2, rwkv_k_k, rwkv_k_a, rwkv_r_k,
              rwkv_lnx_g, rwkv_lnx_b, w_branch_a, w_branch_b, w_out, ln2_g, ln2_b,
              ffn2_up, ffn2_down, ln3_g, ln3_b)
    y_prompt = x_prompt
    y_sample = x_sample
    for l in range(DEPTH):
        layer_params = [p[l] for p in params]
        y_prompt = encoder_layer(y_prompt, *layer_params)
        y_sample = encoder_layer(y_sample, *layer_params)
    return (y_prompt, y_sample)
```

```python
import numpy as np
import ml_dtypes
from contextlib import ExitStack
import concourse.bass as bass
import concourse.mybir as mybir
from concourse.bass_utils import run_bass_kernel_spmd

F32 = mybir.dt.float32
BF16 = mybir.dt.bfloat16
AF = mybir.ActivationFunctionType
ALU = mybir.AluOpType
AX = mybir.AxisListType

EPOCH = 8192
D = 1024
FF = 2816
NJ = FF // 128
NTOK = 4096
TT_ = 512
ALPHA = 2.0 ** 0.25
LN_EPS = 1e-5
NEXT = 6912


class Buf:
    __slots__ = ("name", "last_w", "readers", "excl")

    def __init__(self, name="", excl=False):
        self.name = name
        self.last_w = None
        self.readers = []
        self.excl = excl


class Op:
    __slots__ = ("eng", "seq", "fn", "waits", "needed", "mark", "dma", "dsem", "dval", "prewait")

    def __init__(self, eng, seq, fn):
        self.eng = eng
        self.seq = seq
        self.fn = fn
        self.waits = []
        self.needed = False
        self.mark = 0
        self.dma = False
        self.dsem = None
        self.dval = 0
        self.prewait = None


class Eng:
    def __init__(self, name, kind):
        self.name = name
        self.kind = kind
        self.ops = []
        self.waited = {}
        self.sems = []
        self.dma_sems = []
        self.dma_count = 0


class Prog:
    def __init__(self, nc, ndma=24):
        self.nc = nc
        self.pe = Eng("pe", "pe")
        self.act = Eng("act", "c")
        self.dve = Eng("dve", "c")
        self.pool = Eng("pool", "c")
        self.sp = Eng("sp", "q")
        self.engs = [self.pe, self.act, self.dve, self.pool, self.sp]
        self.ndma = ndma
        self.dma_engs = [self.sp, self.pool]
        self.all_dma_ops = []

    def emit(self, eng, fn, reads=(), writes=(), dma=False):
        op = Op(eng, len(eng.ops), fn)
        deps = []
        for b in reads:
            if b.last_w is not None:
                deps.append(b.last_w)
            if b.excl:
                deps.extend(r for r in b.readers if r.eng is not eng)
        for b in writes:
            if b.last_w is not None:
                deps.append(b.last_w)
            deps.extend(b.readers)
        best = {}
        for d in deps:
            if d is op:
                continue
            if d.dma:
                key = ("d", d.dsem)
                val = d.dval
            else:
                if d.eng is eng and eng.kind == "pe":
                    continue
                key = ("e", d.eng.name)
                val = d.seq + 1
            if val > best.get(key, (0, None))[0]:
                best[key] = (val, d)
        for key, (val, d) in best.items():
            if eng.waited.get(key, 0) >= val:
                continue
            eng.waited[key] = val
            op.waits.append(d)
            d.needed = True
        if dma:
            op.dma = True
            k = eng.dma_count
            eng.dma_count += 1
            j = k % self.ndma
            op.dsem = (eng.name, j)
            op.dval = 16 * (k // self.ndma + 1)
            if k >= self.ndma:
                op.prewait = 16 * (k // self.ndma)
            self.all_dma_ops.append(op)
        eng.ops.append(op)
        for b in writes:
            b.last_w = op
            b.readers = []
        for b in reads:
            if b.last_w is not op:
                b.readers.append(op)
        return op

    def barrier(self):
        lasts = []
        for e in self.engs:
            if e.kind == "q":
                continue
            for op in reversed(e.ops):
                if op.fn is not None:
                    lasts.append(op)
                    break
        latest = {}
        for op in self.all_dma_ops:
            latest[op.dsem] = op
        for e in self.engs:
            op = Op(e, len(e.ops), None)
            for d in lasts:
                if d.eng is e and e.kind == "pe":
                    continue
                key = ("e", d.eng.name)
                val = d.seq + 1
                if e.waited.get(key, 0) >= val:
                    continue
                e.waited[key] = val
                op.waits.append(d)
                d.needed = True
            for d in latest.values():
                key = ("d", d.dsem)
                if e.waited.get(key, 0) >= d.dval:
                    continue
                e.waited[key] = d.dval
                op.waits.append(d)
            e.ops.append(op)

    def finalize(self, stack):
        nc = self.nc
        for e in self.engs:
            m = 0
            for op in e.ops:
                if op.needed and not op.dma:
                    m += 1
                    op.mark = m
            nsem = (m + EPOCH - 1) // EPOCH
            e.sems = [stack.enter_context(nc.semaphore(f"s_{e.name}_{i}")) for i in range(max(nsem, 1))]
        for e in self.dma_engs:
            if e.dma_count:
                e.dma_sems = [stack.enter_context(nc.semaphore(f"d_{e.name}_{i}"))
                              for i in range(min(self.ndma, e.dma_count))]
        engmap = {e.name: e for e in self.engs}

        def replay(e, h):
            for op in e.ops:
                for d in op.waits:
                    if d.dma:
                        de = engmap[d.dsem[0]]
                        h.wait_ge(de.dma_sems[d.dsem[1]], d.dval)
                    else:
                        mk = d.mark - 1
                        h.wait_ge(d.eng.sems[mk // EPOCH], mk % EPOCH + 1)
                if op.fn is None:
                    continue
                if op.dma:
                    if op.prewait:
                        h.wait_ge(e.dma_sems[op.dsem[1]], op.prewait)
                    ins = op.fn(h)
                    ins.then_inc(e.dma_sems[op.dsem[1]], 16)
                else:
                    ins = op.fn(h)
                    if op.needed:
                        mk = op.mark - 1
                        ins.then_inc(e.sems[mk // EPOCH], 1)

        block = stack.enter_context(nc.Block())

        @block.tensor
        def _(h):
            replay(self.pe, h)

        @block.scalar
        def _(h):
            replay(self.act, h)

        @block.vector
        def _(h):
            replay(self.dve, h)

        @block.gpsimd
        def _(h):
            replay(self.pool, h)

        @block.sync
        def _(h):
            replay(self.sp, h)


class V:
    __slots__ = ("ap", "bs")

    def __init__(self, ap, bs):
        self.ap = ap
        self.bs = bs if isinstance(bs, (list, tuple)) else [bs]


class TT:
    def __init__(self, t, b):
        self.t = t
        self.b = b

    def __getitem__(self, idx):
        return V(self.t[idx], self.b)

    def v(self, ap):
        return V(ap, self.b)


class TTS:
    def __init__(self, t, name):
        self.t = t
        self.bs = [Buf(f"{name}_{i}") for i in range(4)]
        self.b = self.bs

    def __getitem__(self, idx):
        if isinstance(idx, tuple) and len(idx) > 1 and isinstance(idx[1], int):
            return V(self.t[idx], [self.bs[idx[1]]])
        return V(self.t[idx], self.bs)


def _bufs(*vs):
    out = []
    for v in vs:
        if isinstance(v, V):
            out.extend(v.bs)
    return out


def _a(x):
    return x.ap if isinstance(x, V) else x


class K:
    def __init__(self, nc, stack, debug=(), feed=()):
        self.feed = set(feed)
        self.nc = nc
        self.st = stack
        self.P = Prog(nc)
        self.debug = set(debug)
        self.n = 0
        self.ph = stack

    def new_phase(self):
        self.P.barrier()
        if self.ph is not self.st:
            self.ph.close()
        self.ph = ExitStack()
        self.phase_id = getattr(self, "phase_id", 0) + 1

    def sb(self, name, shape, dt):
        t = self.ph.enter_context(self.nc.sbuf_tensor(f"s{getattr(self, 'phase_id', 0)}_" + name, list(shape), dt))
        return TT(t, Buf(name))

    def psum(self, name):
        t = self.st.enter_context(self.nc.psum_tensor(name, [128, 512], F32))
        return TT(t, Buf(name, excl=True))

    def dram(self, name, shape, dt):
        kind = "ExternalOutput" if name in self.debug else "Internal"
        if name in self.feed:
            kind = "ExternalInput"
        return self.nc.dram_tensor(name, list(shape), dt, kind=kind).ap()

    def inp(self, name, shape, dt=F32):
        return self.nc.dram_tensor(name, list(shape), dt, kind="ExternalInput").ap()

    def outp(self, name, shape, dt=F32):
        return self.nc.dram_tensor(name, list(shape), dt, kind="ExternalOutput").ap()

    def E(self, name):
        return getattr(self.P, name)

    def mm(self, out, lhsT, rhs, start=True, stop=True):
        self.P.emit(self.P.pe, lambda h: h.matmul(out.ap, lhsT=lhsT.ap, rhs=rhs.ap, start=start, stop=stop),
                    _bufs(lhsT, rhs), _bufs(out))

    def tr(self, out, in_, ident):
        self.P.emit(self.P.pe, lambda h: h.transpose(out=out.ap, in_=in_.ap, identity=ident.ap),
                    _bufs(in_, ident), _bufs(out))

    def act(self, out, in_, func, scale=1.0, bias=0.0, accum=None):
        def fn(h):
            kw = {}
            if accum is not None:
                kw["accum_out"] = accum.ap
            return h.activation(out=out.ap, in_=in_.ap, func=func, bias=_a(bias), scale=_a(scale), **kw)
        self.P.emit(self.P.act, fn, _bufs(in_, scale, bias), _bufs(out, accum))

    def tt(self, eng, out, a, b, op):
        self.P.emit(self.E(eng), lambda h: h.tensor_tensor(out=out.ap, in0=a.ap, in1=b.ap, op=op),
                    _bufs(a, b), _bufs(out))

    def ts(self, eng, out, a, s1, op0, s2=None, op1=None):
        def fn(h):
            if op1 is None:
                return h.tensor_scalar(out=out.ap, in0=a.ap, scalar1=_a(s1), scalar2=None, op0=op0)
            return h.tensor_scalar(out=out.ap, in0=a.ap, scalar1=_a(s1), scalar2=_a(s2), op0=op0, op1=op1)
        self.P.emit(self.E(eng), fn, _bufs(a, s1, s2), _bufs(out))

    def stt(self, out, in0, scalar, in1, op0, op1):
        self.P.emit(self.P.dve, lambda h: h.scalar_tensor_tensor(out=out.ap, in0=in0.ap, scalar=_a(scalar),
                                                                 in1=in1.ap, op0=op0, op1=op1),
                    _bufs(in0, scalar, in1), _bufs(out))

    def cp(self, eng, out, in_):
        if eng == "act":
            self.P.emit(self.P.act, lambda h: h.copy(out=out.ap, in_=in_.ap), _bufs(in_), _bufs(out))
        else:
            self.P.emit(self.E(eng), lambda h: h.tensor_copy(out=out.ap, in_=in_.ap), _bufs(in_), _bufs(out))

    def dma(self, out, in_):
        self.P.emit(self.P.sp, lambda h: h.dma_start(out=out.ap, in_=in_.ap), _bufs(in_), _bufs(out), dma=True)

    def store(self, out, in_):
        self.P.emit(self.P.pool, lambda h: h.dma_start(out=out.ap, in_=in_.ap), _bufs(in_), _bufs(out), dma=True)

    def memset(self, eng, out, val):
        self.P.emit(self.E(eng), lambda h: h.memset(out.ap, val), [], _bufs(out))

    def scan(self, out, d0, d1):
        self.P.emit(self.P.dve, lambda h: h.tensor_tensor_scan(out=out.ap, data0=d0.ap, data1=d1.ap, initial=0.0,
                                                               op0=ALU.mult, op1=ALU.add),
                    _bufs(d0, d1), _bufs(out))

    def recip(self, out, in_):
        self.P.emit(self.P.dve, lambda h: h.reciprocal(out=out.ap, in_=in_.ap), _bufs(in_), _bufs(out))

    def reduce(self, out, in_, op=ALU.add):
        self.P.emit(self.P.dve, lambda h: h.tensor_reduce(out=out.ap, in_=in_.ap, axis=AX.X, op=op),
                    _bufs(in_), _bufs(out))

    def mmx(self, out, lhsT, rhs, start, stop):
        self.P.emit(self.P.pe, lambda h: h.matmul(out.ap, lhsT=lhsT.ap, rhs=rhs.ap, start=start, stop=stop,
                                                  skip_group_check=True),
                    _bufs(lhsT, rhs), _bufs(out))

    def sbp(self, name, shape, dt):
        t = self.st.enter_context(self.nc.sbuf_tensor("s_" + name, list(shape), dt))
        return TT(t, Buf(name))

    def sb4(self, name, shape, dt):
        t = self.ph.enter_context(self.nc.sbuf_tensor(f"s{getattr(self, 'phase_id', 0)}_" + name, list(shape), dt))
        return TTS(t, name)

    def rr(self, engs):
        self.n += 1
        return engs[self.n % len(engs)]


LD = 0.5 * float(np.exp(-0.5))
P_MUP, P_MUN, P_W0, P_A0, P_KK, P_KA, P_RK, P_LG, P_LB, P_BM, NPP = 0, 14, 28, 36, 44, 48, 52, 56, 60, 64, 65
D_C0, D_HW0, D_HA0, D_HK, D_OMHK = 0, 14, 22, 30, 34


def build_program(ntok=NTOK, debug=(), phases=(0, 1, 2, 3, 4, 5), feed=()):
    nc = bass.Bass("TRN2", target_bir_lowering=False)
    ntile = ntok // TT_
    with ExitStack() as st:
        k = K(nc, st, debug, feed)
        NCH = ntok // 128
        x_in = k.inp("x", [ntok, D])
        w1u_in = k.inp("w1u", [D, 2 * FF])
        w1d_in = k.inp("w1d", [FF, D])
        w2u_in = k.inp("w2u", [D, 2 * FF])
        w2d_in = k.inp("w2d", [FF, D])
        win_in = k.inp("win", [D, NEXT])
        wa_in = k.inp("wa", [512, D])
        wb_in = k.inp("wb", [512, D])
        wo_in = k.inp("wo", [D, D])
        lnp_in = k.inp("lnp", [6, D])
        ident_in = k.inp("ident", [128, 128])
        cos_in = k.inp("cosT", [128, ntok])
        sin_in = k.inp("sinT", [128, ntok])
        pp_in = k.inp("pp", [128, NPP])
        w2_in = k.inp("w2", [64, 2, 512])
        a2_in = k.inp("a2", [64, 2, 512])
        g2_in = k.inp("g2", [128, 512])
        rmask_in = k.inp("rmask", [128, 512])
        bones_in = k.inp("bones", [128, 128])
        hsel_in = k.inp("hsel", [2, 128, 4, 128])
        mask1_in = k.inp("mask1", [2, 128, 512])
        maska_in = k.inp("maska", [2, 128, 512])
        rint_in = k.inp("rint", [4, 128, 128])
        rdq_in = k.inp("rdq", [2, 4, 128, 512])
        rkd_in = k.inp("rkd", [128, 8])
        y_out = k.outp("y", [ntok, D])

        W1U = k.dram("W1U", [2 * FF // 512, 128, 8, 512], BF16)
        W1D = k.dram("W1D", [128, NJ, D], BF16)
        W2U = k.dram("W2U", [2 * FF // 512, 128, 8, 512], BF16)
        W2D = k.dram("W2D", [128, NJ, D], BF16)
        WIN = k.dram("WIN", [(NEXT + 511) // 512, 128, 8, 512], BF16)
        WAs = k.dram("WAs", [128, 4, D], BF16)
        WBs = k.dram("WBs", [128, 4, D], BF16)
        WOs = k.dram("WOs", [128, 8, D], BF16)
        X1 = k.dram("X1", [ntok, D], F32)
        X2 = k.dram("X2", [ntok, D], F32)
        UA = k.dram("UA", [1792, ntok], BF16)
        QR = k.dram("QR", [512, ntok], BF16)
        KR = k.dram("KR", [512, ntok], BF16)
        SG = k.dram("SG", [512, ntok], BF16)
        TG = k.dram("TG", [2048, ntok], BF16)
        VR = k.dram("VR", [ntok, 512], BF16)
        AR = k.dram("AR", [2, 4, 128, NCH, 256], BF16)
        BK = k.dram("BK", [2, 4, 128, NCH, 256], BF16)
        BKT = k.dram("BKT", [2, ntok, 2, 512], BF16)
        VT = k.dram("VT", [ntok, 512], BF16)
        GG = k.dram("GG", [512, ntok], F32)
        BVG = k.dram("BVG", [512, ntok], F32)
        YD = [k.dram("YF", [ntok, 512], F32), k.dram("YB", [ntok, 512], F32)]
        OD = k.dram("OD", [ntok, 512], F32)
        bW = {n: Buf(n) for n in ("W1U", "W1D", "W2U", "W2D", "WIN", "WAs", "WBs", "WOs")}
        tb = lambda n: [Buf(f"{n}_{i}") for i in range(ntile)]
        bX1, bX2, bUA, bQR, bKR, bSG, bTG, bVR = (tb(n) for n in ("X1", "X2", "UA", "QR", "KR", "SG", "TG", "VR"))
        bAR = [tb(f"AR{d}") for d in range(2)]
        bBK = [tb(f"BK{d}") for d in range(2)]
        bBKT = [tb(f"BKT{d}") for d in range(2)]
        bVT, bGG, bBVG, bOD = tb("VT"), tb("GG"), tb("BVG"), tb("OD")
        bYD = [tb(f"YD{d}") for d in range(2)]

        ps = [k.psum(f"ps{i}") for i in range(8)]

        idf = k.sbp("idf", [128, 128], F32)
        idb = k.sbp("idb", [128, 128], BF16)
        k.dma(idf[:], V(ident_in, []))
        k.cp("dve", idb[:], idf[:])
        etot = k.sbp("etot", [128, 2, 4, NCH], F32)
        ppt = k.sbp("ppt", [128, NPP], F32)
        k.dma(ppt[:], V(pp_in, []))

        late_rounds = []
        if 0 in phases:
            k.new_phase()
            NSTG = 5
            stg = [k.sb(f"stg{i}", [128, 4096], F32) for i in range(NSTG)]
            stb = [k.sb(f"stb{i}", [128, 4096], BF16) for i in range(NSTG)]
            rnd = [0]

            rounds = []

            def cast_round(src_ap, dst_ap, shape, dbuf):
                rounds.append((src_ap, dst_ap, shape, dbuf))

            def emit_rounds():
                early = [r for r in rounds if r[3] in (bW["W1U"], bW["W1D"], bW["WIN"])]
                late_rounds.extend(r for r in rounds if r[3] not in (bW["W1U"], bW["W1D"], bW["WIN"]))
                rounds[:] = early

                def views(r):
                    i = r % NSTG
                    a, b = rounds[r][2]
                    sv = stg[i].t[:, 0:a * b].rearrange("p (a b) -> p a b", a=a)
                    bv = stb[i].t[:, 0:a * b].rearrange("p (a b) -> p a b", a=a)
                    return i, sv, bv
                LA = NSTG - 1
                for r in range(min(LA, len(rounds))):
                    i, sv, bv = views(r)
                    k.dma(V(sv, stg[i].b), V(rounds[r][0], []))
                for r in range(len(rounds)):
                    i, sv, bv = views(r)
                    k.cp(["dve", "act"][r % 2], V(bv, stb[i].b), V(sv, stg[i].b))
                    k.store(V(rounds[r][1], rounds[r][3]), V(bv, stb[i].b))
                    if r + LA < len(rounds):
                        i2, sv2, bv2 = views(r + LA)
                        k.dma(V(sv2, stg[i2].b), V(rounds[r + LA][0], []))

            def cast_kn(src, dst, ncols, dbuf):
                v = src.rearrange("(k p) n -> p k n", p=128)
                for si, c0 in enumerate(range(0, ncols, 512)):
                    w = min(512, ncols - c0)
                    cast_round(v[:, :, c0:c0 + w], dst[si, :, :, 0:w], (8, w), dbuf)

            def cast_jn(src, dst, dbuf):
                v = src.rearrange("(j p) n -> p j n", p=128)
                nj_ = v.shape[1]
                for j0 in range(0, nj_, 4):
                    nj = min(4, nj_ - j0)
                    cast_round(v[:, j0:j0 + nj, :], dst[:, j0:j0 + nj, :], (nj, D), dbuf)

            cast_kn(w1u_in, W1U, 2 * FF, bW["W1U"])
            cast_jn(w1d_in, W1D, bW["W1D"])
            cast_kn(win_in, WIN, NEXT, bW["WIN"])
            cast_jn(wa_in, WAs, bW["WAs"])
            cast_jn(wb_in, WBs, bW["WBs"])
            cast_jn(wo_in, WOs, bW["WOs"])
            cast_kn(w2u_in, W2U, 2 * FF, bW["W2U"])
            cast_jn(w2d_in, W2D, bW["W2D"])
            emit_rounds()

        def transpose_to(srcb, dstT, pbank=0):
            for s in range(4):
                pb = ps[pbank + (s % 2)]
                pv = pb.t[:].bitcast(BF16)
                for kk in range(8):
                    k.tr(V(pv[:, kk * 128:(kk + 1) * 128], pb.b), srcb[:, s, kk * 128:(kk + 1) * 128], idb[:])
                k.cp("act" if s % 2 else "dve", dstT[:, :, s * 128:(s + 1) * 128],
                     V(pv.rearrange("p (a b) -> p a b", a=8), pb.b))

        def layer_norm_tiles(zt, xo, xob, g_bc, b_bc, stats, mv, rstd, nmr, eps, geng="dve"):
            for s in range(4):
                k.P.emit(k.P.dve, lambda h, s=s: h.bn_aggr(out=mv.t[:, s, :],
                                                           in_=stats.t[:, s, :, :].rearrange("p a b -> p (a b)")),
                         [stats.b], [mv.b])
            k.ts("dve", rstd[:], mv[:, :, 1], eps, ALU.add)
            k.act(rstd[:], rstd[:], AF.Sqrt)
            k.recip(rstd[:], rstd[:])
            k.stt(nmr[:], mv[:, :, 0], -1.0, rstd[:], ALU.mult, ALU.mult)
            for s in range(4):
                k.act(zt[:, s, :], zt[:, s, :], AF.Identity, scale=rstd[:, s:s + 1], bias=nmr[:, s:s + 1])
                k.tt(geng, zt[:, s, :], zt[:, s, :], g_bc[:], ALU.mult)
                if xob is not None:
                    k.tt("dve", xob[:, s, :], zt[:, s, :], b_bc[:], ALU.add)
                k.tt("pool", xo[:, s, :], zt[:, s, :], b_bc[:], ALU.add)

        def bn_stats(stats, s, hf, zt):
            k.P.emit(k.P.dve, lambda h: h.bn_stats(out=stats.t[:, s, hf, :], in_=zt.t[:, s, hf * 512:(hf + 1) * 512]),
                     zt[:, s, 0:1].bs, [stats.b])

        def ffn_phase(tag, src, src_bufs, WU, bWU, WD, bWD, lrow, post, alloc_extra, want_bf=True):
            k.new_phase()
            wd = k.sb("wd", [128, NJ, D], BF16)
            g_bc = k.sb("g_bc", [128, D], F32)
            b_bc = k.sb("b_bc", [128, D], F32)

            def load_resident():
                k.dma(wd[:], V(WD, bWD))
                k.dma(g_bc[:], V(lnp_in[lrow:lrow + 1, :].partition_broadcast(128), []))
                k.dma(b_bc[:], V(lnp_in[lrow + 1:lrow + 2, :].partition_broadcast(128), []))
            xtok = [k.sb4(f"xtok{i}", [128, 4, D], F32) for i in range(3)]
            xbf = k.sb("xbf", [128, 4, D], BF16)
            x1b = k.sb4("x1b", [128, 4, D], BF16)
            xT = k.sb("xT", [128, 8, TT_], BF16)
            actT = [k.sb(f"actT{j}", [128, TT_], BF16) for j in range(NJ)]
            wu = [k.sb(f"wu{i}", [128, 8, 512], BF16) for i in range(3)]
            th = [k.sb(f"th{i}", [128, TT_], F32) for i in range(2)]
            sgt = [k.sb(f"sgt{i}", [128, TT_], F32) for i in range(2)]
            stats = k.sb("stats", [128, 4, 2, 6], F32)
            mv = k.sb("mv", [128, 4, 2], F32)
            rstd = k.sb("rstd", [128, 4], F32)
            nmr = k.sb("nmr", [128, 4], F32)
            ups = [0]
            extra = alloc_extra()

            def next_w():
                w = wu[ups[0] % 3]
                ups[0] += 1
                return w

            def load_x(ti):
                t0 = ti * TT_
                k.dma(xtok[ti % 3][:], V(src[t0:t0 + TT_, :].rearrange("(s p) d -> p s d", p=128),
                                         [src_bufs[ti]] if src_bufs else []))

            def stage_a0(ti):
                xt = xtok[ti % 3]
                for s in range(4):
                    k.cp("act" if s % 2 else "dve", xbf[:, s, :], xt[:, s, :])

                transpose_to(xbf, xT)

            def stage_a(ti):
                xt = xtok[ti % 3]
                for s in range(NJ // 2):
                    w = next_w()
                    k.dma(w[:], V(WU[s], bWU))
                    if ti == 0 and s == 1:
                        load_resident()
                    for jj in range(2):
                        j = 2 * s + jj
                        pg, pu = ps[2 + 2 * (j % 2)], ps[3 + 2 * (j % 2)]
                        for kk in range(8):
                            k.mm(pg[:], w[:, kk, jj * 128:(jj + 1) * 128], xT[:, kk, :], kk == 0, kk == 7)
                        for kk in range(8):
                            k.mm(pu[:], w[:, kk, 256 + jj * 128:256 + (jj + 1) * 128], xT[:, kk, :], kk == 0, kk == 7)
                        k.act(th[j % 2][:], pg[:], AF.Tanh, scale=0.5)
                        k.stt(sgt[j % 2][:], th[j % 2][:], 1.0, pg[:], ALU.add, ALU.mult)
                        k.tt("dve", actT[j][:], sgt[j % 2][:], pu[:], ALU.mult)
                for s in range(4):
                    for hf in range(2):
                        pz = ps[6 + (2 * s + hf) % 2]
                        for j in range(NJ):
                            k.mm(pz[:], actT[j][:, s * 128:(s + 1) * 128], wd[:, j, hf * 512:(hf + 1) * 512],
                                 j == 0, j == NJ - 1)
                        k.stt(xt[:, s, hf * 512:(hf + 1) * 512], xt[:, s, hf * 512:(hf + 1) * 512], 4.0 * ALPHA,
                              pz[:], ALU.mult, ALU.add)
                        bn_stats(stats, s, hf, xt)

            def stage_l(ti):
                xt = xtok[ti % 3]
                layer_norm_tiles(xt, xt, x1b if want_bf else None, g_bc, b_bc, stats, mv, rstd, nmr, 16.0 * LN_EPS)

            def stage_b(ti):
                post(ti, xtok[ti % 3], x1b, xT, next_w, th, sgt, extra)

            load_x(0)
            stage_a0(0)
            stage_a(0)
            for ti in range(1, min(3, ntile)):
                load_x(ti)
            if ntile > 1:
                stage_a0(1)
            stage_l(0)
            for ti in range(1, ntile):
                stage_a(ti)
                stage_b(ti - 1)
                if ti + 2 < ntile:
                    load_x(ti + 2)
                if ti + 1 < ntile:
                    stage_a0(ti + 1)
                stage_l(ti)
            stage_b(ntile - 1)

        def p1_alloc():
            e = {}
            e["cs"] = [k.sb(f"cs{i}", [128, TT_], F32) for i in range(2)]
            e["sn"] = [k.sb(f"sn{i}", [128, TT_], F32) for i in range(2)]
            e["ost"] = [k.sb(f"ost{i}", [128, 4, TT_], BF16) for i in range(3)]
            e["ostn"] = [0]
            return e

        def p1_post(ti, x1, x1b, x1T, next_w, th, sgt, e):
            t0 = ti * TT_
            cs, sn, ost, ostn = e["cs"], e["sn"], e["ost"], e["ostn"]
            t1, t2 = th, sgt
            k.store(V(X1[t0:t0 + TT_, :].rearrange("(s p) d -> p s d", p=128), bX1[ti]), x1[:])
            transpose_to(x1b, x1T)
            k.dma(cs[ti % 2][:], V(cos_in[:, t0:t0 + TT_], []))
            k.dma(sn[ti % 2][:], V(sin_in[:, t0:t0 + TT_], []))
            pcnt = [0]

            def proj_chunk(w, cl):
                pb = ps[2 + pcnt[0] % 6]
                pcnt[0] += 1
                for kk in range(8):
                    k.mm(pb[:], w[:, kk, cl * 128:(cl + 1) * 128], x1T[:, kk, :], kk == 0, kk == 7)
                return pb

            def flush(o, dst, row0, nrows, dbuf):
                k.store(V(dst[row0:row0 + nrows * 128, t0:t0 + TT_].rearrange("(a p) t -> p a t", p=128), dbuf),
                      o[:, 0:nrows, :])

            def nost():
                o = ost[ostn[0] % 3]
                ostn[0] += 1
                return o

            cur_sl = [-1]
            wcur = [None]

            slabs = {}

            def load_slab(sl):
                if sl in slabs or sl > 12:
                    return
                w = next_w()
                k.dma(w[:], V(WIN[sl], bW["WIN"]))
                slabs[sl] = w

            def get_chunk(c):
                sl = c // 4
                if sl != cur_sl[0]:
                    load_slab(sl)
                    load_slab(sl + 1)
                    cur_sl[0] = sl
                    wcur[0] = slabs[sl]
                return proj_chunk(wcur[0], c % 4)

            c = 0
            for (row0, ncl) in ((0, 4), (512, 4), (1024, 4), (1536, 2)):
                o = nost()
                for cl in range(ncl):
                    pb = get_chunk(c)
                    c += 1
                    k.cp("act" if cl % 2 else "dve", o[:, cl, :], pb[:])
                flush(o, UA, row0, ncl, bUA[ti])
            for (dst, dbuf) in ((QR, bQR[ti]), (KR, bKR[ti])):
                o = nost()
                for hh in range(4):
                    pq = get_chunk(c)
                    pqs = get_chunk(c + 1)
                    c += 2
                    k.tt("dve", t1[hh % 2][:], pq[:], cs[ti % 2][:], ALU.mult)
                    k.tt("dve", t2[hh % 2][:], pqs[:], sn[ti % 2][:], ALU.mult)
                    k.tt("pool", o[:, hh, :], t1[hh % 2][:], t2[hh % 2][:], ALU.add)
                flush(o, dst, 0, 4, dbuf)
            o = nost()
            for hh in range(4):
                pg = get_chunk(c)
                c += 1
                k.act(th[hh % 2][:], pg[:], AF.Tanh, scale=0.5)
                k.stt(o[:, hh, :], th[hh % 2][:], 1.0, pg[:], ALU.add, ALU.mult)
            flush(o, SG, 0, 4, bSG[ti])
            for gg in range(4):
                o = nost()
                for hh in range(4):
                    pg = get_chunk(c)
                    c += 1
                    k.act(o[:, hh, :], pg[:], AF.Tanh, scale=0.5)
                flush(o, TG, gg * 512, 4, bTG[ti])
            assert c == 50
            load_slab(12)
            w = slabs[12]
            w2_ = next_w()
            k.dma(w2_[:, :, 0:256], V(WIN[13, :, :, 0:256], bW["WIN"]))
            o = nost()
            for s in range(4):
                pb = ps[2 + pcnt[0] % 6]
                pcnt[0] += 1
                for kk in range(8):
                    k.mm(pb[:, 0:256], x1T[:, kk, s * 128:(s + 1) * 128], w[:, kk, 256:512], kk == 0, kk == 7)
                for kk in range(8):
                    k.mm(pb[:, 256:512], x1T[:, kk, s * 128:(s + 1) * 128], w2_[:, kk, 0:256], kk == 0, kk == 7)
                k.cp("act" if s % 2 else "dve", o[:, s, :], pb[:])
            k.store(V(VR[t0:t0 + TT_, :].rearrange("(s p) c -> p s c", p=128), bVR[ti]), o[:])

        if 1 in phases:
            ffn_phase("f1", x_in, None, W1U, bW["W1U"], W1D, bW["W1D"], 0, p1_post, p1_alloc)

        if 2 in phases:
            k.new_phase()
            dp = k.sb("dp", [128, 40], F32)
            k.tt("dve", dp[:, 0:14], ppt[:, P_MUP:P_MUP + 14], ppt[:, P_MUN:P_MUN + 14], ALU.add)
            k.ts("dve", dp[:, 0:14], dp[:, 0:14], -1.0, ALU.mult, 1.0, ALU.add)
            k.ts("dve", dp[:, D_HW0:D_HW0 + 8], ppt[:, P_W0:P_W0 + 8], 0.5, ALU.mult)
            k.ts("dve", dp[:, D_HA0:D_HA0 + 8], ppt[:, P_A0:P_A0 + 8], 0.5, ALU.mult)
            k.ts("dve", dp[:, D_HK:D_HK + 4], ppt[:, P_KA:P_KA + 4], 0.5, ALU.mult)
            k.ts("dve", dp[:, D_OMHK:D_OMHK + 4], ppt[:, P_KA:P_KA + 4], -0.5, ALU.mult, 1.0, ALU.add)
            stgs = k.sb("stgs", [128, 1024], F32)
            w2b = k.sb("w2b", [64, 2, 512], BF16)
            a2b = k.sb("a2b", [128, 2, 512], BF16)
            g2b = k.sb("g2b", [128, 512], BF16)
            bones = k.sb("bones", [128, 128], BF16)
            hsel = k.sb("hsel", [128, 4, 128], BF16)
            hselT = k.sb("hselT", [128, 4, 128], F32)
            rmask = k.sb("rmask", [128, 512], F32)
            k.dma(rmask[:], V(rmask_in, []))
            k.dma(stgs[0:64, :], V(w2_in.rearrange("p d n -> p (d n)"), []))
            k.cp("dve", V(w2b.t[:].rearrange("p d n -> p (d n)"), w2b.b), stgs[0:64, :])
            k.dma(stgs[64:128, :], V(a2_in.rearrange("p d n -> p (d n)"), []))
            k.cp("dve", V(a2b.t[64:128].rearrange("p d n -> p (d n)"), a2b.b), stgs[64:128, :])
            k.dma(stgs[:, 0:512], V(g2_in, []))
            k.cp("dve", g2b[:], stgs[:, 0:512])
            k.dma(stgs[:, 0:128], V(bones_in, []))
            k.cp("dve", bones[:], stgs[:, 0:128])
            k.dma(stgs[:, 0:512], V(hsel_in[0].rearrange("p c r -> p (c r)"), []))
            k.cp("dve", V(hsel.t[:].rearrange("p c r -> p (c r)"), hsel.b), stgs[:, 0:512])
            k.dma(V(hselT.t[:].rearrange("p c r -> p (c r)"), hselT.b), V(hsel_in[1].rearrange("p c r -> p (c r)"), []))

            uh = [k.sb(f"uh{i}", [128, 14, 514], BF16) for i in range(2)]
            S = [k.sb(f"S{c}", [128, 512], F32) for c in range(14)]
            tx = k.sb("tx", [128, 512], BF16)
            xab = k.sb("xab", [128, 512], BF16)
            sigxf = k.sb("sigxf", [128, 512], F32)
            sigx = k.sb("sigx", [128, 512], BF16)
            kk2 = [k.sb(f"kk2_{i}", [128, 512], BF16) for i in range(2)]
            rn = k.sb("rn", [128, 512], F32)
            kkn = [k.sb(f"kkn{c}", [128, 512], F32) for c in range(4)]
            vsb = [k.sb(f"vsb{c}", [128, 512], BF16) for c in range(4)]
            NB = 3
            ones1 = k.sb("ones1", [128, 1], F32)
            k.memset("pool", ones1[:], 1.0)
            tw = [k.sb(f"tw{i}", [128, 512], F32) for i in range(NB)]
            tw1 = [k.sb(f"tw1_{i}", [128, 512], F32) for i in range(NB)]
            cum = [k.sb(f"cum{i}", [128, 512], F32) for i in range(NB)]
            cumx = [k.sb(f"cumx{i}", [128, 512], F32) for i in range(NB)]
            E1 = [k.sb(f"E1_{i}", [128, 512], F32) for i in range(2)]
            E2 = [k.sb(f"E2_{i}", [128, 512], F32) for i in range(2)]
            E3 = [k.sb(f"E3_{i}", [128, 512], F32) for i in range(2)]
            ta = [k.sb(f"ta{i}", [128, 512], F32) for i in range(NB)]
            ff_ = [k.sb(f"ff{i}", [128, 512], F32) for i in range(NB)]
            b2 = [k.sb(f"b2_{i}", [128, 512], F32) for i in range(NB)]
            kdir = [[k.sb(f"kdir{p}_{d}", [128, 512], F32) for d in range(2)] for p in range(2)]
            ARt = [k.sb(f"ARt{i}", [128, 4, 2, 128], BF16) for i in range(NB)]
            BKt = [k.sb(f"BKt{i}", [128, 4, 2, 128], BF16) for i in range(NB)]
            BKTt = [k.sb(f"BKTt{d}", [128, 4, 2, 512], BF16) for d in range(2)]
            VTt = k.sb("VTt", [128, 4, 512], BF16)
            kds = k.sb("kds", [128, 512], F32)
            rb = k.sb("rb", [128, 512], BF16)
            bv = k.sb("bv", [128, 512], F32)
            ggt = [k.sb(f"ggt{i}", [128, 512], F32) for i in range(2)]
            bvgt = [k.sb(f"bvgt{i}", [128, 512], F32) for i in range(2)]
            pc = [0]

            def nps():
                pc[0] += 1
                return ps[pc[0] % 8]

            def v4(t):
                return V(t.t[:].rearrange("p (n x) -> p n x", n=4), t.b)

            def load_u(ti):
                t0 = ti * TT_
                U = uh[ti % 2]
                lo, hi = max(t0 - 1, 0), min(t0 + 513, ntok)
                bl = [bUA[j] for j in range(max(ti - 1, 0), min(ti + 2, ntile))]
                k.dma(U[:, :, lo - (t0 - 1):hi - (t0 - 1)], V(UA[:, lo:hi].rearrange("(c p) t -> p c t", p=128), bl))
                if t0 == 0:
                    k.memset("pool", U[:, :, 0:1], 0.0)
                if t0 + TT_ == ntok:
                    k.memset("pool", U[:, :, 513:514], 0.0)
                if t0 == ntok // 2:
                    k.ts("pool", U[:, :, 0:1], U[:, :, 0:1], ppt[:, P_BM:P_BM + 1], ALU.mult)
                if t0 + TT_ == ntok // 2:
                    k.ts("pool", U[:, :, 513:514], U[:, :, 513:514], ppt[:, P_BM:P_BM + 1], ALU.mult)

            load_u(0)
            for ti in range(ntile):
                t0 = ti * TT_
                U = uh[ti % 2]
                if ti + 1 < ntile:
                    load_u(ti + 1)
                def shift(tj, cs_):
                    Uj = uh[tj % 2]
                    for c in cs_:
                        k.act(S[c][:], Uj[:, c, 1:513], AF.Identity, scale=dp[:, D_C0 + c:D_C0 + c + 1])
                        k.stt(S[c][:], Uj[:, c, 0:512], ppt[:, P_MUP + c:P_MUP + c + 1], S[c][:], ALU.mult, ALU.add)
                        k.stt(S[c][:], Uj[:, c, 2:514], ppt[:, P_MUN + c:P_MUN + c + 1], S[c][:], ALU.mult, ALU.add)

                if ti == 0:
                    shift(0, range(14))
                    for c in range(4):
                        k.cp("pool", vsb[c][:], S[8 + c][:])
                k.act(tx[0:64, :], S[12][0:64, :], AF.Tanh)
                k.cp("pool", xab[64:128, :], S[12][64:128, :])
                k.act(sigxf[:], S[13][:], AF.Tanh, scale=0.5)
                k.ts("pool", sigx[:], sigxf[:], 0.5, ALU.mult, 0.5, ALU.add)
                pn = nps()
                for c in range(4):
                    k.act(kk2[c % 2][:], S[4 + c][:], AF.Square, scale=ppt[:, P_KK + c:P_KK + c + 1])
                    k.mm(pn[:], hsel[:, c, :], kk2[c % 2][:], c == 0, c == 3)
                k.ts("dve", rn[:], pn[:], 1e-24, ALU.max)
                k.act(rn[:], rn[:], AF.Sqrt)
                k.recip(rn[:], rn[:])
                for c in range(4):
                    pr = nps()
                    k.mm(pr[:], hselT[:, c, :], rn[:])
                    k.stt(kkn[c][:], S[4 + c][:], ppt[:, P_KK + c:P_KK + c + 1], pr[:], ALU.mult, ALU.mult)
                def h1(j):
                    c, d = j // 2, j % 2
                    i_ = j % NB
                    pz = nps()
                    k.mm(pz[:], w2b[0:64, d, c * 128:(c + 1) * 128], tx[0:64, :])
                    k.act(tw[i_][:], pz[:], AF.Tanh, scale=0.5, bias=dp[:, D_HW0 + d * 4 + c:D_HW0 + d * 4 + c + 1])
                    k.act(tw1[i_][:], tw[i_][:], AF.Identity, bias=ones1[:, 0:1])
                    k.scan(cum[i_][:], rmask[:], tw1[i_][:])
                    k.tt("pool", cumx[i_][:], cum[i_][:], tw1[i_][:], ALU.subtract)
                    pa = nps()
                    k.mm(pa[:], a2b[64:128, d, c * 128:(c + 1) * 128], xab[64:128, :])
                    k.act(ta[i_][:], pa[:], AF.Tanh, scale=0.5, bias=dp[:, D_HA0 + d * 4 + c:D_HA0 + d * 4 + c + 1])
                    k.act(ff_[i_][:], ta[i_][:], AF.Identity, scale=dp[:, D_HK + c:D_HK + c + 1],
                          bias=dp[:, D_OMHK + c:D_OMHK + c + 1])
                    k.tt("pool", kdir[c % 2][d][:], ff_[i_][:], S[4 + c][:], ALU.mult)
                    k.stt(b2[i_][:], ta[i_][:], 1.0, kkn[c][:], ALU.add, ALU.mult)

                def h2(j):
                    c, d = j // 2, j % 2
                    i_ = j % NB
                    if d == 0:
                        k.act(E1[j % 2][:], cum[i_][:], AF.Exp, scale=-LD)
                        k.act(E2[j % 2][:], cum[i_][:], AF.Exp, scale=LD)
                        k.act(E3[j % 2][:], cumx[i_][:], AF.Exp, scale=-LD)
                    else:
                        k.act(E1[j % 2][:], cumx[i_][:], AF.Exp, scale=LD)
                        k.act(E2[j % 2][:], cumx[i_][:], AF.Exp, scale=-LD)
                        k.act(E3[j % 2][:], cum[i_][:], AF.Exp, scale=LD)
                    k.act(etot[:, d, c, ti * 4:(ti + 1) * 4],
                          V(cum[i_].t[:].rearrange("p (n x) -> p n x", n=4)[:, :, 127], cum[i_].b), AF.Exp, scale=-LD)
                    At, Bt = ARt[i_], BKt[i_]
                    k.stt(At[:, :, 0, :], v4(kkn[c]), -1.0, v4(E3[j % 2]), ALU.mult, ALU.mult)
                    k.tt("pool", At[:, :, 1, :], v4(S[c]), v4(E1[j % 2]), ALU.mult)
                    k.stt(Bt[:, :, 0, :], v4(b2[i_]), 0.5, v4(E2[j % 2]), ALU.mult, ALU.mult)
                    k.tt("pool", Bt[:, :, 1, :], v4(kdir[c % 2][d]), v4(E2[j % 2]), ALU.mult)
                    k.dma(V(AR[d, c, :, ti * 4:(ti + 1) * 4, :].rearrange("p n (j x) -> p n j x", j=2), bAR[d][ti]), At[:])
                    k.dma(V(BK[d, c, :, ti * 4:(ti + 1) * 4, :].rearrange("p n (j x) -> p n j x", j=2), bBK[d][ti]), Bt[:])
                    pt = nps()
                    pv = pt.t[:].bitcast(BF16)
                    for n in range(4):
                        for jj in range(2):
                            k.tr(V(pv[:, (n * 2 + jj) * 128:(n * 2 + jj + 1) * 128], pt.b), Bt[:, n, jj, :], idb[:])
                    k.cp("act", BKTt[d][:, :, :, c * 128:(c + 1) * 128],
                         V(pv.rearrange("p (n j x) -> p n j x", n=4, j=2), pt.b))

                def tail(c):
                    k.tt("pool", kds[:], kdir[c % 2][0][:], kdir[c % 2][1][:], ALU.add)
                    k.stt(rb[:], S[c][:], ppt[:, P_RK + c:P_RK + c + 1], kds[:], ALU.mult, ALU.mult)
                    pb_ = nps()
                    k.mm(pb_[:], bones[:], rb[:])
                    k.tt("dve", bv[:], pb_[:], S[8 + c][:], ALU.mult)
                    pg = nps()
                    k.mm(pg[:], g2b[:, c * 128:(c + 1) * 128], sigx[:])
                    k.act(ggt[c % 2][:], pg[:], AF.Identity, scale=ppt[:, P_LG + c:P_LG + c + 1])
                    k.stt(bvgt[c % 2][:], bv[:], ppt[:, P_LB + c:P_LB + c + 1], pg[:], ALU.add, ALU.mult)
                    k.dma(V(GG[c * 128:(c + 1) * 128, t0:t0 + TT_], bGG[ti]), ggt[c % 2][:])
                    k.dma(V(BVG[c * 128:(c + 1) * 128, t0:t0 + TT_], bBVG[ti]), bvgt[c % 2][:])
                    pt = nps()
                    pv = pt.t[:].bitcast(BF16)
                    for n in range(4):
                        k.tr(V(pv[:, n * 128:(n + 1) * 128], pt.b), vsb[c][:, n * 128:(n + 1) * 128], idb[:])
                    k.cp("act", VTt[:, :, c * 128:(c + 1) * 128], V(pv[:, 0:512].rearrange("p (n x) -> p n x", n=4), pt.b))

                def next_shift(c):
                    if ti + 1 < ntile:
                        shift(ti + 1, (c, 4 + c, 8 + c))
                        k.cp("pool", vsb[c][:], S[8 + c][:])

                h1(0)
                for j in range(1, 8):
                    h1(j)
                    h2(j - 1)
                    if (j - 1) % 2 == 1:
                        tail((j - 1) // 2)
                        next_shift((j - 1) // 2)
                h2(7)
                tail(3)
                next_shift(3)
                if ti + 1 < ntile:
                    shift(ti + 1, (12, 13))
                for d in range(2):
                    k.dma(V(BKT[d, t0:t0 + TT_, :, :].rearrange("(n p) j x -> p n j x", p=128), bBKT[d][ti]), BKTt[d][:])
                k.dma(V(VT[t0:t0 + TT_, :].rearrange("(n p) x -> p n x", p=128), bVT[ti]), VTt[:])
            k.ts("dve", etot[:, :, :, NCH // 2 - 1], etot[:, :, :, NCH // 2 - 1], ppt[:, P_BM:P_BM + 1], ALU.mult)

        if 2 in phases:
            k.new_phase()
            stgm = k.sb("stgm", [128, 512], F32)
            mask1 = [k.sb(f"mask1_{d}", [128, 512], BF16) for d in range(2)]
            maska = [k.sb(f"maska_{d}", [128, 512], BF16) for d in range(2)]
            for d in range(2):
                k.dma(stgm[:], V(mask1_in[d], []))
                k.cp("dve", mask1[d][:], stgm[:])
                k.dma(stgm[:], V(maska_in[d], []))
                k.cp("dve", maska[d][:], stgm[:])
            SBDf = [k.sb(f"SBDf{d}", [128, 4, 128], F32) for d in range(2)]
            SBDb = [[k.sb(f"SBDb{p}_{d}", [128, 4, 128], BF16) for d in range(2)] for p in range(2)]
            for d in range(2):
                k.memset("pool", SBDf[d][:], 0.0)
                for p in range(2):
                    k.memset("pool", SBDb[p][d][:], 0.0)
            ARs = [[k.sb(f"ARs{p}_{d}", [128, 4, 256], BF16) for d in range(2)] for p in range(2)]
            BKs = [[k.sb(f"BKs{p}_{d}", [128, 4, 256], BF16) for d in range(2)] for p in range(2)]
            BKTs = [[k.sb(f"BKTs{p}_{d}", [128, 2, 512], BF16) for d in range(2)] for p in range(2)]
            VTs = [[k.sb(f"VTs{p}_{d}", [128, 512], BF16) for d in range(2)] for p in range(2)]
            M1 = [[[k.sb(f"M1_{p}_{d}_{h}", [128, 512], BF16) for h in range(8)] for d in range(2)] for p in range(2)]
            A0 = [[k.sb(f"A0_{d}_{hp}", [128, 512], BF16) for hp in range(2)] for d in range(2)]
            CB = [[[k.sb(f"CB{l}_{d}_{q}", [128, 512], BF16) for q in range(4)] for d in range(2)] for l in range(2)]
            Tb = [[[[k.sb(f"Tb{p}_{l}_{d}_{q}", [128, 256], BF16) for q in range(4)] for d in range(2)]
                   for l in range(2)] for p in range(2)]
            Xb = [k.sb(f"Xb{d}", [128, 512], BF16) for d in range(2)]
            m1tmp = [k.sb(f"m1tmp{i}", [128, 512], BF16) for i in range(2)]
            Ub = [k.sb(f"Ub{d}", [128, 512], BF16) for d in range(2)]
            Gt = [k.sb(f"Gt{d}", [128, 4, 128], F32) for d in range(2)]
            Ysb = [[k.sb(f"Ysb{p}_{d}", [128, 512], F32) for d in range(2)] for p in range(2)]

            def chunk_of(i, d):
                return i if d == 0 else NCH - 1 - i

            def pre_segments(i):
                par = i % 2
                segs = []

                def seg_stage1(d):
                    n = chunk_of(i, d)
                    tl = n // 4
                    k.dma(ARs[par][d][:], V(AR[d, :, :, n, :].rearrange("c p x -> p c x"), bAR[d][tl]))
                    k.dma(BKs[par][d][:], V(BK[d, :, :, n, :].rearrange("c p x -> p c x"), bBK[d][tl]))
                    k.dma(BKTs[par][d][:], V(BKT[d, n * 128:(n + 1) * 128, :, :], bBKT[d][tl]))
                    k.dma(VTs[par][d][:], V(VT[n * 128:(n + 1) * 128, :], bVT[tl]))
                    a_, b_ = ARs[par][d], BKs[par][d]
                    for h in range(8):
                        c, p0 = h // 2, 64 * (h % 2)
                        s1 = ps[h % 4]
                        k.mm(s1[:, 0:256], b_[p0:p0 + 64, c, 0:128], a_[p0:p0 + 64, c, 0:256])
                        k.mm(s1[:, 256:512], b_[p0:p0 + 64, c, 128:256], a_[p0:p0 + 64, c, 0:256])
                        if h in (3, 7):
                            k.cp("act", m1tmp[h // 4][:], s1[:])
                            k.tt("pool", M1[par][d][h][:], m1tmp[h // 4][:], mask1[d][:], ALU.mult)
                        else:
                            k.tt("dve", M1[par][d][h][:], s1[:], mask1[d][:], ALU.mult)
                    for h in range(8):
                        c, p0 = h // 2, 64 * (h % 2)
                        k.mm(ps[h % 2][:, c * 128:(c + 1) * 128], a_[p0:p0 + 64, c, 0:128], b_[p0:p0 + 64, c, 0:128])
                    for hp in range(2):
                        k.tt("dve", A0[d][hp][:], ps[hp][:], maska[d][:], ALU.mult)
                    for h in range(8):
                        q, hp = h // 2, h % 2
                        k.tt("pool", Tb[par][0][d][q][:, hp * 128:(hp + 1) * 128], M1[par][d][h][:, 0:128], idb[:], ALU.add)

                def seg_level(lv):
                    items = [(d, q) for d in range(2) for q in range(4)]

                    def chain(i):
                        d, q = items[i]
                        bank = ps[i % 4]
                        for hp in range(2):
                            h = 2 * q + hp
                            if lv == 0:
                                Ak = A0[d][hp][:, q * 128:(q + 1) * 128]
                                Bk = M1[par][d][h][:, 0:128]
                            else:
                                Ak = CB[lv % 2][d][q][:, hp * 128:(hp + 1) * 128]
                                Bk = CB[lv % 2][d][q][:, 256 + hp * 128:256 + (hp + 1) * 128]
                            k.mm(bank[:, hp * 128:(hp + 1) * 128], Bk, Ak)
                            if lv < 5:
                                k.mm(bank[:, 256 + hp * 128:256 + (hp + 1) * 128], Ak, Bk)
                        nc_ = 512 if lv < 5 else 256
                        k.cp("act", CB[(lv + 1) % 2][d][q][:, 0:nc_], bank[:, 0:nc_])

                    def tprod(i):
                        d, q = items[i]
                        bankT = ps[4 + i % 2]
                        Tc, Tn = Tb[par][lv % 2][d][q], Tb[par][(lv + 1) % 2][d][q]
                        if True:
                            for hp in range(2):
                                k.mm(bankT[:, hp * 128:(hp + 1) * 128], CB[(lv + 1) % 2][d][q][:, hp * 128:(hp + 1) * 128],
                                     Tc[:, hp * 128:(hp + 1) * 128])
                            k.tt("dve", Tn[:], bankT[:, 0:256], Tc[:], ALU.add)
                        else:
                            for hp in range(2):
                                k.mm(bankT[:, hp * 128:(hp + 1) * 128], idb[:], Tc[:, hp * 128:(hp + 1) * 128], True, False)
                                k.mm(bankT[:, hp * 128:(hp + 1) * 128], CB[(lv + 1) % 2][d][q][:, hp * 128:(hp + 1) * 128],
                                     Tc[:, hp * 128:(hp + 1) * 128], False, True)
                            k.cp("act", Tn[:], bankT[:, 0:256])

                    for i in range(8):
                        chain(i)
                        if i >= 2:
                            tprod(i - 2)
                    tprod(6)
                    tprod(7)

                segs.append(lambda: seg_stage1(0))
                segs.append(lambda: seg_stage1(1))
                for lv in range(6):
                    segs.append(lambda lv=lv: seg_level(lv))
                return segs

            def stage_X(i):
                par = i % 2
                for d in range(2):
                    px = ps[6 + d]
                    for c in range(4):
                        k.mmx(px[:, c * 128:(c + 1) * 128], ARs[par][d][:, c, 0:128], SBDb[par][d][:, c, :], True, False)
                        for hp in range(2):
                            h = 2 * c + hp
                            k.mmx(px[:, h * 64:(h + 1) * 64], M1[par][d][h][:, 256:384], VTs[par][d][:, h * 64:(h + 1) * 64],
                                  False, True)
                    k.cp("act", Xb[d][:], px[:])

            def stage_U(i):
                par = i % 2
                for d in range(2):
                    pu = ps[6 + d]
                    for h in range(8):
                        q, hp = h // 2, h % 2
                        k.mm(pu[:, h * 64:(h + 1) * 64], Tb[par][0][d][q][:, hp * 128:(hp + 1) * 128], Xb[d][:, h * 64:(h + 1) * 64])
                    k.cp("dve", Ub[d][:], pu[:])

            def stage_S(i):
                par = i % 2
                if i == NCH - 1:
                    return
                for d in range(2):
                    pd = ps[6 + d]
                    for h in range(8):
                        c, hp = h // 2, h % 2
                        o = pd[hp * 64:(hp + 1) * 64, c * 128 + hp * 64:c * 128 + (hp + 1) * 64]
                        k.mmx(o, BKTs[par][d][:, 0, h * 64:(h + 1) * 64], Ub[d][:, h * 64:(h + 1) * 64], True, False)
                        k.mmx(o, BKTs[par][d][:, 1, h * 64:(h + 1) * 64], VTs[par][d][:, h * 64:(h + 1) * 64], False, True)
                    ne = i if d == 0 else NCH - 2 - i
                    pdv = pd.t[:].rearrange("p (c x) -> p c x", c=4)
                    for hp in range(2):
                        sl = slice(hp * 64, (hp + 1) * 64)
                        ev = etot.t[sl, d, :, ne:ne + 1].to_broadcast([64, 4, 64])
                        k.tt("dve", Gt[d][sl, :, sl], SBDf[d][sl, :, sl], V(pdv[sl, :, sl], pd.b), ALU.add)
                        k.tt("dve", SBDf[d][sl, :, sl], Gt[d][sl, :, sl], V(ev, etot.b), ALU.mult)
                        k.cp("pool", SBDb[1 - par][d][sl, :, sl], SBDf[d][sl, :, sl])

            def stage_Y(i):
                par = i % 2
                for d in range(2):
                    n = chunk_of(i, d)
                    py = ps[6 + d]
                    for c in range(4):
                        k.mmx(py[:, c * 128:(c + 1) * 128], ARs[par][d][:, c, 128:256], SBDb[par][d][:, c, :], True, False)
                        for hp in range(2):
                            h = 2 * c + hp
                            k.mmx(py[:, h * 64:(h + 1) * 64], M1[par][d][h][:, 128:256], Ub[d][:, h * 64:(h + 1) * 64], False, False)
                            k.mmx(py[:, h * 64:(h + 1) * 64], M1[par][d][h][:, 384:512], VTs[par][d][:, h * 64:(h + 1) * 64],
                                  False, True)
                    k.cp("act", Ysb[par][d][:], py[:])
                    k.store(V(YD[d][n * 128:(n + 1) * 128, :], bYD[d][n // 4]), Ysb[par][d][:])

            stgL = [k.sb(f"stgL{i}", [128, 4096], F32) for i in range(2)]
            stbL = [k.sb(f"stbL{i}", [128, 4096], BF16) for i in range(2)]
            lr = [0]

            def late_cast(n):
                for _ in range(n):
                    if lr[0] >= len(late_rounds):
                        return
                    src_ap, dst_ap, (a_, b_), dbuf = late_rounds[lr[0]]
                    i_ = lr[0] % 2
                    lr[0] += 1
                    sv = stgL[i_].t[:, 0:a_ * b_].rearrange("p (a b) -> p a b", a=a_)
                    bv = stbL[i_].t[:, 0:a_ * b_].rearrange("p (a b) -> p a b", a=a_)
                    k.dma(V(sv, stgL[i_].b), V(src_ap, []))
                    k.cp("act", V(bv, stbL[i_].b), V(sv, stgL[i_].b))
                    k.store(V(dst_ap, dbuf), V(bv, stbL[i_].b))

            for sg in pre_segments(0):
                sg()
            per_step = (len(late_rounds) + NCH - 1) // NCH
            for i in range(NCH):
                segs = pre_segments(i + 1) if i + 1 < NCH else []

                def run(a, b):
                    for sg in segs[a:b]:
                        sg()
                stage_X(i)
                late_cast(per_step)
                run(0, 2)
                stage_U(i)
                run(2, 4)
                stage_S(i)
                run(4, 6)
                stage_Y(i)
                run(6, 8)
            late_cast(len(late_rounds))

        pcg = [0]

        def gps():
            pcg[0] += 1
            return ps[pcg[0] % 8]

        if 3 in phases:
            k.new_phase()
            GC = [float((1.0 - 2.0 ** (-5.0 - h)) ** 128) for h in range(4)]
            Kall = k.sb("Kall", [128, 4, ntok], BF16)
            Vall = k.sb("Vall", [128, NCH, 512], BF16)
            k.dma(Kall[:], V(KR.rearrange("(h p) t -> p h t", p=128), bKR))
            k.dma(Vall[:], V(VR.rearrange("(n p) c -> p n c", p=128), bVR))
            RfS = [k.sb(f"RfS{h}", [128, NCH, 128], BF16) for h in range(4)]
            RbS = [k.sb(f"RbS{h}", [128, NCH, 128], BF16) for h in range(4)]
            stg2 = k.sb("stg2", [128, 512], F32)
            rint = [k.sb(f"rint{h}", [128, 128], BF16) for h in range(4)]
            rdq = [[k.sb(f"rdq{d}_{h}", [128, 512], BF16) for h in range(4)] for d in range(2)]
            rkd = k.sb("rkd", [128, 8], F32)
            k.dma(rkd[:], V(rkd_in, []))
            for h in range(4):
                k.dma(stg2[:, 0:128], V(rint_in[h], []))
                k.cp("dve", rint[h][:], stg2[:, 0:128])
                for d in range(2):
                    k.dma(stg2[:], V(rdq_in[d, h], []))
                    k.cp("dve", rdq[d][h][:], stg2[:])
            Rst = [[k.sb(f"Rst{d}_{h}", [128, 128], F32) for h in range(4)] for d in range(2)]
            for d in range(2):
                for h in range(4):
                    k.memset("pool", Rst[d][h][:], 0.0)
            kf = [k.sb(f"kf{i}", [128, 128], BF16) for i in range(4)]
            kfi = 0
            for d in range(2):
                order = range(NCH) if d == 0 else range(NCH - 1, -1, -1)
                RS = RfS if d == 0 else RbS
                for n in order:
                    for h in range(4):
                        pt = gps()
                        pv = pt.t[:].bitcast(BF16)
                        k.tr(V(pv[:, 0:128], pt.b), Kall[:, h, n * 128:(n + 1) * 128], idb[:])
                        kfc = kf[kfi % 4]
                        kfi += 1
                        k.ts("dve", kfc[:], V(pv[:, 0:128], pt.b), rkd[:, d * 4 + h:d * 4 + h + 1], ALU.mult)
                        pk = gps()
                        k.mm(pk[:, 0:128], kfc[:], Vall[:, n, h * 128:(h + 1) * 128])
                        k.cp("pool", RS[h][:, n, :], Rst[d][h][:])
                        k.stt(Rst[d][h][:], Rst[d][h][:], GC[h], pk[:, 0:128], ALU.mult, ALU.add)
                        if (d == 0 and n + 1 == NCH // 2) or (d == 1 and n == NCH // 2):
                            k.ts("dve", Rst[d][h][:], Rst[d][h][:], ppt[:, P_BM:P_BM + 1], ALU.mult)
            Qt = [k.sb(f"Qt{i}", [128, 4, TT_], BF16) for i in range(2)]
            qfb = [[k.sb(f"qfb{d}_{h}", [128, TT_], BF16) for h in range(4)] for d in range(2)]
            sT = [k.sb(f"sT{i}", [128, 128], BF16) for i in range(4)]
            osb = [k.sb(f"osb{i}", [128, 4, 512], F32) for i in range(2)]
            sti = 0
            for ti in range(ntile):
                t0 = ti * TT_
                Q = Qt[ti % 2]
                k.dma(Q[:], V(QR[:, t0:t0 + TT_].rearrange("(h p) t -> p h t", p=128), bQR[ti]))
                for h in range(4):
                    for d in range(2):
                        k.tt("pool" if d == 0 else "dve", qfb[d][h][:], Q[:, h, :], rdq[d][h][:], ALU.mult)
                o_ = osb[ti % 2]
                for s in range(4):
                    n = ti * 4 + s
                    po = gps()
                    for h in range(4):
                        pst = gps()
                        k.mm(pst[:, 0:128], Kall[:, h, n * 128:(n + 1) * 128], Q[:, h, s * 128:(s + 1) * 128])
                        sc_ = sT[sti % 4]
                        sti += 1
                        k.tt("dve", sc_[:], pst[:, 0:128], rint[h][:], ALU.mult)
                        k.mmx(po[:, h * 128:(h + 1) * 128], sc_[:], Vall[:, n, h * 128:(h + 1) * 128], True, False)
                        k.mmx(po[:, h * 128:(h + 1) * 128], qfb[0][h][:, s * 128:(s + 1) * 128], RfS[h][:, n, :], False, False)
                        k.mmx(po[:, h * 128:(h + 1) * 128], qfb[1][h][:, s * 128:(s + 1) * 128], RbS[h][:, n, :], False, True)
                    k.cp("act", o_[:, s, :], po[:])
                k.store(V(OD[t0:t0 + TT_, :].rearrange("(s p) c -> p s c", p=128), bOD[ti]), o_[:])

        if 4 in phases:
            k.new_phase()
            WA = k.sb("WA", [128, 4, D], BF16)
            WB = k.sb("WB", [128, 4, D], BF16)
            WO = k.sb("WO", [128, 8, D], BF16)
            k.dma(WA[:], V(WAs, bW["WAs"]))
            k.dma(WB[:], V(WBs, bW["WBs"]))
            k.dma(WO[:], V(WOs, bW["WOs"]))
            g_bc = k.sb("g_bc", [128, D], F32)
            b_bc = k.sb("b_bc", [128, D], F32)
            k.dma(g_bc[:], V(lnp_in[2:3, :].partition_broadcast(128), []))
            k.dma(b_bc[:], V(lnp_in[3:4, :].partition_broadcast(128), []))
            bufA = k.sb("bufA", [128, 4, 512], F32)
            bufB = k.sb("bufB", [128, 4, 512], F32)
            bufO = k.sb("bufO", [128, 4, 512], F32)
            sq = k.sb("sq", [128, 4, 512], F32)
            ynb = k.sb("ynb", [128, 4, 512], BF16)
            onb = k.sb("onb", [128, 4, 512], BF16)
            s1 = k.sb("s1", [128, 32], F32)
            s2 = k.sb("s2", [128, 32], F32)
            mean = k.sb("mean", [128, 32], F32)
            msq = k.sb("msq", [128, 32], F32)
            var = k.sb("var", [128, 32], F32)
            ggc = [k.sb(f"ggc{i}", [128, 512], F32) for i in range(2)]
            bvgc = [k.sb(f"bvgc{i}", [128, 512], F32) for i in range(2)]
            tmpf = [k.sb(f"tmpf{i}", [128, 512], F32) for i in range(2)]
            SGt = k.sb("SGt", [128, 4, 512], BF16)
            TGt = k.sb("TGt", [128, 16, 512], BF16)
            yaT = [k.sb(f"yaT{i}", [128, 4, 512], BF16) for i in range(2)]
            ybT = [k.sb(f"ybT{i}", [128, 4, 512], BF16) for i in range(2)]
            mT = k.sb("mT", [128, 8, 512], BF16)
            m1 = [k.sb(f"m1_{i}", [128, 512], F32) for i in range(2)]
            m2 = [k.sb(f"m2_{i}", [128, 512], F32) for i in range(2)]
            x1t = [k.sb4(f"x1t{i}", [128, 4, D], F32) for i in range(2)]
            stats = k.sb("stats", [128, 4, 2, 6], F32)
            mv = k.sb("mv", [128, 4, 2], F32)
            rstd = k.sb("rstd", [128, 4], F32)
            nmr = k.sb("nmr", [128, 4], F32)

            st2 = [[k.sb(f"hn{i}_{j}", [128, 32], F32) for j in range(5)] for i in range(2)]

            def head_norm2(specs):
                R = []
                for i, (buf, sqb, nh, hd, eps, outb) in enumerate(specs):
                    v3 = V(buf.t[:].rearrange("p s (h x) -> p (s h) x", x=hd), buf.b)
                    q3 = V(sqb.t[:].rearrange("p s (h x) -> p (s h) x", x=hd), sqb.b)
                    o3 = V(outb.t[:].rearrange("p s (h x) -> p (s h) x", x=hd), outb.b)
                    R.append((buf, sqb, nh, hd, eps, v3, q3, o3) + tuple(st2[i]))
                for (buf, sqb, nh, hd, eps, v3, q3, o3, s1_, s2_, mean_, msq_, var_) in R:
                    k.reduce(s1_[:, 0:nh], v3)
                for (buf, sqb, nh, hd, eps, v3, q3, o3, s1_, s2_, mean_, msq_, var_) in R:
                    k.act(sqb[:], buf[:], AF.Square)
                for (buf, sqb, nh, hd, eps, v3, q3, o3, s1_, s2_, mean_, msq_, var_) in R:
                    k.reduce(s2_[:, 0:nh], q3)
                for (buf, sqb, nh, hd, eps, v3, q3, o3, s1_, s2_, mean_, msq_, var_) in R:
                    k.ts("dve", mean_[:, 0:nh], s1_[:, 0:nh], 1.0 / hd, ALU.mult)
                    k.tt("dve", msq_[:, 0:nh], mean_[:, 0:nh], mean_[:, 0:nh], ALU.mult)
                    k.stt(var_[:, 0:nh], s2_[:, 0:nh], 1.0 / hd, msq_[:, 0:nh], ALU.mult, ALU.subtract)
                    k.ts("dve", var_[:, 0:nh], var_[:, 0:nh], eps, ALU.add)
                for (buf, sqb, nh, hd, eps, v3, q3, o3, s1_, s2_, mean_, msq_, var_) in R:
                    k.act(var_[:, 0:nh], var_[:, 0:nh], AF.Sqrt)
                for (buf, sqb, nh, hd, eps, v3, q3, o3, s1_, s2_, mean_, msq_, var_) in R:
                    k.recip(var_[:, 0:nh], var_[:, 0:nh])
                for (buf, sqb, nh, hd, eps, v3, q3, o3, s1_, s2_, mean_, msq_, var_) in R:
                    mb = V(mean_.t[:, 0:nh].unsqueeze(2).to_broadcast([128, nh, hd]), mean_.b)
                    k.tt("pool", v3, v3, mb, ALU.subtract)
                for bi, (buf, sqb, nh, hd, eps, v3, q3, o3, s1_, s2_, mean_, msq_, var_) in enumerate(R):
                    rb_ = V(var_.t[:, 0:nh].unsqueeze(2).to_broadcast([128, nh, hd]), var_.b)
                    k.tt("pool" if bi == 0 else "dve", o3, v3, rb_, ALU.mult)

            def tview(ap):
                return ap.rearrange("(s p) c -> p s c", p=128)

            def p3_a(ti):
                t0 = ti * TT_
                ya, yb_ = yaT[ti % 2], ybT[ti % 2]
                k.dma(bufA[:], V(tview(YD[0][t0:t0 + TT_, :]), bYD[0][ti]))
                k.dma(bufB[:], V(tview(YD[1][t0:t0 + TT_, :]), bYD[1][ti]))
                k.dma(bufO[:], V(tview(OD[t0:t0 + TT_, :]), bOD[ti]))
                k.dma(SGt[:], V(SG[:, t0:t0 + TT_].rearrange("(h p) t -> p h t", p=128), bSG[ti]))
                k.tt("dve", bufA[:], bufA[:], bufB[:], ALU.add)
                head_norm2([(bufA, bufB, 32, 64, 64e-5, ynb), (bufO, sq, 16, 128, 1e-6, onb)])
                for c in range(4):
                    k.dma(ggc[c % 2][:], V(GG[c * 128:(c + 1) * 128, t0:t0 + TT_], bGG[ti]))
                    k.dma(bvgc[c % 2][:], V(BVG[c * 128:(c + 1) * 128, t0:t0 + TT_], bBVG[ti]))
                    pt = gps()
                    pv = pt.t[:].bitcast(BF16)
                    for s in range(4):
                        k.tr(V(pv[:, s * 128:(s + 1) * 128], pt.b), ynb[:, s, c * 128:(c + 1) * 128], idb[:])
                    k.tt("dve", tmpf[c % 2][:], V(pv[:, 0:512], pt.b), ggc[c % 2][:], ALU.mult)
                    k.tt("pool", ya[:, c, :], tmpf[c % 2][:], bvgc[c % 2][:], ALU.add)
                for h in range(4):
                    pt = gps()
                    pv = pt.t[:].bitcast(BF16)
                    for s in range(4):
                        k.tr(V(pv[:, s * 128:(s + 1) * 128], pt.b), onb[:, s, h * 128:(h + 1) * 128], idb[:])
                    k.stt(yb_[:, h, :], V(pv[:, 0:512], pt.b), 0.5, SGt[:, h, :], ALU.mult, ALU.mult)

            def p3_b(ti):
                t0 = ti * TT_
                xt = x1t[ti % 2]
                ya, yb_ = yaT[ti % 2], ybT[ti % 2]
                k.dma(xt[:], V(tview(X1[t0:t0 + TT_, :]), bX1[ti]))
                k.dma(TGt[:], V(TG[:, t0:t0 + TT_].rearrange("(h p) t -> p h t", p=128), bTG[ti]))
                for dc in range(8):
                    pa = gps()
                    for c in range(4):
                        k.mm(pa[:], WA[:, c, dc * 128:(dc + 1) * 128], ya[:, c, :], c == 0, c == 3)
                    pb = gps()
                    for c in range(4):
                        k.mm(pb[:], WB[:, c, dc * 128:(dc + 1) * 128], yb_[:, c, :], c == 0, c == 3)
                    k.stt(m1[dc % 2][:], TGt[:, dc, :], 1.0, pa[:], ALU.add, ALU.mult)
                    k.stt(m2[dc % 2][:], TGt[:, 8 + dc, :], 1.0, pb[:], ALU.add, ALU.mult)
                    k.tt("pool", mT[:, dc, :], m1[dc % 2][:], m2[dc % 2][:], ALU.add)
                for s in range(4):
                    for hf in range(2):
                        pz = gps()
                        for kk in range(8):
                            k.mm(pz[:], mT[:, kk, s * 128:(s + 1) * 128], WO[:, kk, hf * 512:(hf + 1) * 512], kk == 0, kk == 7)
                        k.stt(xt[:, s, hf * 512:(hf + 1) * 512], xt[:, s, hf * 512:(hf + 1) * 512], 2.0 * ALPHA,
                              pz[:], ALU.mult, ALU.add)
                        bn_stats(stats, s, hf, xt)
                layer_norm_tiles(xt, xt, None, g_bc, b_bc, stats, mv, rstd, nmr, 4.0 * LN_EPS, geng="pool")
                k.store(V(tview(X2[t0:t0 + TT_, :]), bX2[ti]), xt[:])

            p3_a(0)
            for ti in range(1, ntile):
                p3_a(ti)
                p3_b(ti - 1)
            p3_b(ntile - 1)

        def p3_post(ti, xo, xob, xT, next_w, th, sgt, e):
            t0 = ti * TT_
            k.store(V(y_out[t0:t0 + TT_, :].rearrange("(s p) d -> p s d", p=128), []), xo[:])

        if 5 in phases:
            ffn_phase("f2", X2, bX2, W2U, bW["W2U"], W2D, bW["W2D"], 4, p3_post, lambda: None, want_bf=False)

        k.new_phase()
        k.P.finalize(st)
        k.ph.close()
    return nc


def _consts(seq_lens):
    ntok = sum(seq_lens)
    pos = np.concatenate([np.arange(L, dtype=np.float64) for L in seq_lens])
    angle = np.repeat(1.0 / (10000.0 ** np.linspace(0.0, 1.0, 64, dtype=np.float64)), 2)
    theta = pos[:, None] * angle[None, :]
    sgn = np.where(np.arange(128) % 2 == 0, -1.0, 1.0)
    c = {}
    c["cosT"] = np.ascontiguousarray(np.cos(theta).T.astype(np.float32))
    c["sinT"] = np.ascontiguousarray((np.sin(theta) * sgn[None, :]).T.astype(np.float32))
    c["ident"] = np.eye(128, dtype=np.float32)
    rm = np.ones((128, 512), np.float32)
    rm[:, ::128] = 0.0
    c["rmask"] = rm
    bo = np.zeros((128, 128), np.float32)
    bo[:64, :64] = 1.0
    bo[64:, 64:] = 1.0
    c["bones"] = bo
    hs = np.zeros((2, 128, 4, 128), np.float32)
    for cc in range(4):
        for hp in range(2):
            hs[0, hp * 64:(hp + 1) * 64, cc, 32 * cc + hp] = 1.0
            hs[1, 32 * cc + hp, cc, hp * 64:(hp + 1) * 64] = 1.0
    c["hsel"] = hs
    s_ = np.arange(128)[:, None]
    t_ = np.arange(128)[None, :]
    gt, ge, lt, le = (t_ > s_), (t_ >= s_), (t_ < s_), (t_ <= s_)
    c["mask1"] = np.stack([np.concatenate([gt, ge, gt, ge], 1), np.concatenate([lt, le, lt, le], 1)]).astype(np.float32)
    c["maska"] = np.stack([np.tile(lt, (1, 4)), np.tile(gt, (1, 4))]).astype(np.float32)
    idx = np.arange(128, dtype=np.float64)
    sc = 128.0 ** -0.5
    rint = np.zeros((4, 128, 128)); rdq = np.zeros((2, 4, 128, 512)); rkd = np.zeros((128, 8))
    for h in range(4):
        lg = np.log(1.0 - 2.0 ** (-5.0 - h))
        rint[h] = sc * np.exp(lg * np.abs(idx[:, None] - idx[None, :]))
        rdq[0, h] = np.tile(np.exp(lg * (idx + 1.0))[None, :], (128, 4))
        rdq[1, h] = np.tile(np.exp(lg * (128.0 - idx))[None, :], (128, 4))
        rkd[:, h] = sc * np.exp(lg * (127.0 - idx))
        rkd[:, 4 + h] = sc * np.exp(lg * idx)
    c["rint"] = rint.astype(np.float32)
    c["rdq"] = rdq.astype(np.float32)
    c["rkd"] = rkd.astype(np.float32)
    return c


def _weights(inp):
    w = {}
    up = inp["ffn1_up"][0]
    perm = []
    for s in range(NJ // 2):
        perm += list(range(s * 256, (s + 1) * 256)) + list(range(FF + s * 256, FF + (s + 1) * 256))
    perm = np.array(perm)
    w["w1u"] = np.ascontiguousarray(up[:, perm])
    w["w1d"] = np.ascontiguousarray(inp["ffn1_down"][0])
    w["w2u"] = np.ascontiguousarray(inp["ffn2_up"][0][:, perm])
    w["w2d"] = np.ascontiguousarray(inp["ffn2_down"][0])
    w["wa"] = np.ascontiguousarray(inp["w_branch_a"][0])
    w["wb"] = np.ascontiguousarray(inp["w_branch_b"][0])
    w["wo"] = np.ascontiguousarray(inp["w_out"][0])
    w["lnp"] = np.ascontiguousarray(np.stack([inp[n][0] for n in ("ln1_g", "ln1_b", "ln2_g", "ln2_b", "ln3_g", "ln3_b")]))
    win = inp["w_in"][0]
    RW = 1792
    swap = np.arange(128) ^ 1
    cols = list(range(RW))
    q0, k0, v0, g0 = RW, RW + 512, RW + 1024, RW + 1536
    for base in (q0, k0):
        for h in range(4):
            cols += list(range(base + h * 128, base + (h + 1) * 128))
            cols += list(base + h * 128 + swap)
    cols += list(range(g0, g0 + 512))
    cols += list(range(RW + 2048, RW + 2048 + 2048))
    cols += list(range(v0, v0 + 512))
    assert len(cols) == NEXT
    w["win"] = np.ascontiguousarray(win[:, np.array(cols)])
    pp = np.zeros((128, NPP), np.float32)
    def cols(v, n):
        return np.ascontiguousarray(np.asarray(v, np.float32).reshape(n, 128).T)
    pp[:, P_MUP:P_MUP + 14] = cols(inp["shift_prev"][0], 14)
    pp[:, P_MUN:P_MUN + 14] = cols(inp["shift_next"][0], 14)
    pp[:, P_W0:P_W0 + 8] = cols(inp["rwkv_w0"][0], 8)
    pp[:, P_A0:P_A0 + 8] = cols(inp["rwkv_a0"][0], 8)
    pp[:, P_KK:P_KK + 4] = cols(inp["rwkv_k_k"][0], 4)
    pp[:, P_KA:P_KA + 4] = cols(inp["rwkv_k_a"][0], 4)
    pp[:, P_RK:P_RK + 4] = cols(inp["rwkv_r_k"][0], 4)
    pp[:, P_LG:P_LG + 4] = cols(inp["rwkv_lnx_g"][0], 4)
    pp[:, P_LB:P_LB + 4] = cols(inp["rwkv_lnx_b"][0], 4)
    w["pp"] = pp
    w["w2"] = np.ascontiguousarray(np.transpose(inp["rwkv_w2"][0], (1, 0, 2)))
    w["a2"] = np.ascontiguousarray(np.transpose(inp["rwkv_a2"][0], (1, 0, 2)))
    w["g2"] = np.ascontiguousarray(inp["rwkv_g2"][0])
    return w


_NC_CACHE = {}


def kernel(**inputs):
    inp = {k_: np.asarray(v) for k_, v in inputs.items()}
    if "nc" not in _NC_CACHE:
        _NC_CACHE["nc"] = build_program()
    nc = _NC_CACHE["nc"]
    w = _weights(inp)
    c1 = _consts([4096])
    c2 = _consts([2048, 2048])
    xp = np.asarray(inp["x_prompt"], np.float32)
    xs = np.asarray(inp["x_sample"], np.float32)
    in_maps = []
    for core in range(8):
        m = dict(w)
        if core < 4:
            m["x"] = np.ascontiguousarray(xp[core])
            m.update(c1)
            bm = 1.0
        else:
            b0 = 2 * (core - 4)
            m["x"] = np.ascontiguousarray(xs[b0:b0 + 2].reshape(4096, D))
            m.update(c2)
            bm = 0.0
        pp = w["pp"].copy()
        pp[:, P_BM] = bm
        m["pp"] = pp
        in_maps.append(m)
    res = run_bass_kernel_spmd(nc, in_maps, core_ids=list(range(8)))
    outs = [np.asarray(r["y"], np.float32) for r in res.results]
    y_prompt = np.stack(outs[:4]).reshape(4, 4096, D)
    y_sample = np.concatenate([o.reshape(2, 2048, D) for o in outs[4:]], axis=0)
    return (y_prompt, y_sample)
```

```python
import numpy as np
import ml_dtypes
from contextlib import ExitStack
import concourse.bass as bass
import concourse.mybir as mybir
from concourse.bass_utils import run_bass_kernel_spmd

F32 = mybir.dt.float32
BF16 = mybir.dt.bfloat16
AF = mybir.ActivationFunctionType
ALU = mybir.AluOpType
AX = mybir.AxisListType

EPOCH = 8192
D = 1024
FF = 2816
NJ = FF // 128
NTOK = 4096
TT_ = 512
ALPHA = 2.0 ** 0.25
LN_EPS = 1e-5
NEXT = 6912


class Buf:
    __slots__ = ("name", "last_w", "readers", "excl")

    def __init__(self, name="", excl=False):
        self.name = name
        self.last_w = None
        self.readers = []
        self.excl = excl


class Op:
    __slots__ = ("eng", "seq", "fn", "waits", "needed", "mark", "dma", "dsem", "dval", "prewait")

    def __init__(self, eng, seq, fn):
        self.eng = eng
        self.seq = seq
        self.fn = fn
        self.waits = []
        self.needed = False
        self.mark = 0
        self.dma = False
        self.dsem = None
        self.dval = 0
        self.prewait = None


class Eng:
    def __init__(self, name, kind):
        self.name = name
        self.kind = kind
        self.ops = []
        self.waited = {}
        self.sems = []
        self.dma_sems = []
        self.dma_count = 0


class Prog:
    def __init__(self, nc, ndma=24):
        self.nc = nc
        self.pe = Eng("pe", "pe")
        self.act = Eng("act", "c")
        self.dve = Eng("dve", "c")
        self.pool = Eng("pool", "c")
        self.sp = Eng("sp", "q")
        self.engs = [self.pe, self.act, self.dve, self.pool, self.sp]
        self.ndma = ndma
        self.dma_engs = [self.sp, self.pool]
        self.all_dma_ops = []

    def emit(self, eng, fn, reads=(), writes=(), dma=False):
        op = Op(eng, len(eng.ops), fn)
        deps = []
        for b in reads:
            if b.last_w is not None:
                deps.append(b.last_w)
            if b.excl:
                deps.extend(r for r in b.readers if r.eng is not eng)
        for b in writes:
            if b.last_w is not None:
                deps.append(b.last_w)
            deps.extend(b.readers)
        best = {}
        for d in deps:
            if d is op:
                continue
            if d.dma:
                key = ("d", d.dsem)
                val = d.dval
            else:
                if d.eng is eng and eng.kind == "pe":
                    continue
                key = ("e", d.eng.name)
                val = d.seq + 1
            if val > best.get(key, (0, None))[0]:
                best[key] = (val, d)
        for key, (val, d) in best.items():
            if eng.waited.get(key, 0) >= val:
                continue
            eng.waited[key] = val
            op.waits.append(d)
            d.needed = True
        if dma:
            op.dma = True
            k = eng.dma_count
            eng.dma_count += 1
            j = k % self.ndma
            op.dsem = (eng.name, j)
            op.dval = 16 * (k // self.ndma + 1)
            if k >= self.ndma:
                op.prewait = 16 * (k // self.ndma)
            self.all_dma_ops.append(op)
        eng.ops.append(op)
        for b in writes:
            b.last_w = op
            b.readers = []
        for b in reads:
            if b.last_w is not op:
                b.readers.append(op)
        return op

    def barrier(self):
        lasts = []
        for e in self.engs:
            if e.kind == "q":
                continue
            for op in reversed(e.ops):
                if op.fn is not None:
                    lasts.append(op)
                    break
        latest = {}
        for op in self.all_dma_ops:
            latest[op.dsem] = op
        for e in self.engs:
            op = Op(e, len(e.ops), None)
            for d in lasts:
                if d.eng is e and e.kind == "pe":
                    continue
                key = ("e", d.eng.name)
                val = d.seq + 1
                if e.waited.get(key, 0) >= val:
                    continue
                e.waited[key] = val
                op.waits.append(d)
                d.needed = True
            for d in latest.values():
                key = ("d", d.dsem)
                if e.waited.get(key, 0) >= d.dval:
                    continue
                e.waited[key] = d.dval
                op.waits.append(d)
            e.ops.append(op)

    def finalize(self, stack):
        nc = self.nc
        for e in self.engs:
            m = 0
            for op in e.ops:
                if op.needed and not op.dma:
                    m += 1
                    op.mark = m
            nsem = (m + EPOCH - 1) // EPOCH
            e.sems = [stack.enter_context(nc.semaphore(f"s_{e.name}_{i}")) for i in range(max(nsem, 1))]
        for e in self.dma_engs:
            if e.dma_count:
                e.dma_sems = [stack.enter_context(nc.semaphore(f"d_{e.name}_{i}"))
                              for i in range(min(self.ndma, e.dma_count))]
        engmap = {e.name: e for e in self.engs}

        def replay(e, h):
            for op in e.ops:
                for d in op.waits:
                    if d.dma:
                        de = engmap[d.dsem[0]]
                        h.wait_ge(de.dma_sems[d.dsem[1]], d.dval)
                    else:
                        mk = d.mark - 1
                        h.wait_ge(d.eng.sems[mk // EPOCH], mk % EPOCH + 1)
                if op.fn is None:
                    continue
                if op.dma:
                    if op.prewait:
                        h.wait_ge(e.dma_sems[op.dsem[1]], op.prewait)
                    ins = op.fn(h)
                    ins.then_inc(e.dma_sems[op.dsem[1]], 16)
                else:
                    ins = op.fn(h)
                    if op.needed:
                        mk = op.mark - 1
                        ins.then_inc(e.sems[mk // EPOCH], 1)

        block = stack.enter_context(nc.Block())

        @block.tensor
        def _(h):
            replay(self.pe, h)

        @block.scalar
        def _(h):
            replay(self.act, h)

        @block.vector
        def _(h):
            replay(self.dve, h)

        @block.gpsimd
        def _(h):
            replay(self.pool, h)

        @block.sync
        def _(h):
            replay(self.sp, h)


class V:
    __slots__ = ("ap", "bs")

    def __init__(self, ap, bs):
        self.ap = ap
        self.bs = bs if isinstance(bs, (list, tuple)) else [bs]


class TT:
    def __init__(self, t, b):
        self.t = t
        self.b = b

    def __getitem__(self, idx):
        return V(self.t[idx], self.b)

    def v(self, ap):
        return V(ap, self.b)


class TTS:
    def __init__(self, t, name):
        self.t = t
        self.bs = [Buf(f"{name}_{i}") for i in range(4)]
        self.b = self.bs

    def __getitem__(self, idx):
        if isinstance(idx, tuple) and len(idx) > 1 and isinstance(idx[1], int):
            return V(self.t[idx], [self.bs[idx[1]]])
        return V(self.t[idx], self.bs)


def _bufs(*vs):
    out = []
    for v in vs:
        if isinstance(v, V):
            out.extend(v.bs)
    return out


def _a(x):
    return x.ap if isinstance(x, V) else x


class K:
    def __init__(self, nc, stack, debug=(), feed=()):
        self.feed = set(feed)
        self.nc = nc
        self.st = stack
        self.P = Prog(nc)
        self.debug = set(debug)
        self.n = 0
        self.ph = stack

    def new_phase(self):
        self.P.barrier()
        if self.ph is not self.st:
            self.ph.close()
        self.ph = ExitStack()
        self.phase_id = getattr(self, "phase_id", 0) + 1

    def sb(self, name, shape, dt):
        t = self.ph.enter_context(self.nc.sbuf_tensor(f"s{getattr(self, 'phase_id', 0)}_" + name, list(shape), dt))
        return TT(t, Buf(name))

    def psum(self, name):
        t = self.st.enter_context(self.nc.psum_tensor(name, [128, 512], F32))
        return TT(t, Buf(name, excl=True))

    def dram(self, name, shape, dt):
        kind = "ExternalOutput" if name in self.debug else "Internal"
        if name in self.feed:
            kind = "ExternalInput"
        return self.nc.dram_tensor(name, list(shape), dt, kind=kind).ap()

    def inp(self, name, shape, dt=F32):
        return self.nc.dram_tensor(name, list(shape), dt, kind="ExternalInput").ap()

    def outp(self, name, shape, dt=F32):
        return self.nc.dram_tensor(name, list(shape), dt, kind="ExternalOutput").ap()

    def E(self, name):
        return getattr(self.P, name)

    def mm(self, out, lhsT, rhs, start=True, stop=True):
        self.P.emit(self.P.pe, lambda h: h.matmul(out.ap, lhsT=lhsT.ap, rhs=rhs.ap, start=start, stop=stop),
                    _bufs(lhsT, rhs), _bufs(out))

    def tr(self, out, in_, ident):
        self.P.emit(self.P.pe, lambda h: h.transpose(out=out.ap, in_=in_.ap, identity=ident.ap),
                    _bufs(in_, ident), _bufs(out))

    def act(self, out, in_, func, scale=1.0, bias=0.0, accum=None):
        def fn(h):
            kw = {}
            if accum is not None:
                kw["accum_out"] = accum.ap
            return h.activation(out=out.ap, in_=in_.ap, func=func, bias=_a(bias), scale=_a(scale), **kw)
        self.P.emit(self.P.act, fn, _bufs(in_, scale, bias), _bufs(out, accum))

    def tt(self, eng, out, a, b, op):
        self.P.emit(self.E(eng), lambda h: h.tensor_tensor(out=out.ap, in0=a.ap, in1=b.ap, op=op),
                    _bufs(a, b), _bufs(out))

    def ts(self, eng, out, a, s1, op0, s2=None, op1=None):
        def fn(h):
            if op1 is None:
                return h.tensor_scalar(out=out.ap, in0=a.ap, scalar1=_a(s1), scalar2=None, op0=op0)
            return h.tensor_scalar(out=out.ap, in0=a.ap, scalar1=_a(s1), scalar2=_a(s2), op0=op0, op1=op1)
        self.P.emit(self.E(eng), fn, _bufs(a, s1, s2), _bufs(out))

    def stt(self, out, in0, scalar, in1, op0, op1):
        self.P.emit(self.P.dve, lambda h: h.scalar_tensor_tensor(out=out.ap, in0=in0.ap, scalar=_a(scalar),
                                                                 in1=in1.ap, op0=op0, op1=op1),
                    _bufs(in0, scalar, in1), _bufs(out))

    def cp(self, eng, out, in_):
        if eng == "act":
            self.P.emit(self.P.act, lambda h: h.copy(out=out.ap, in_=in_.ap), _bufs(in_), _bufs(out))
        else:
            self.P.emit(self.E(eng), lambda h: h.tensor_copy(out=out.ap, in_=in_.ap), _bufs(in_), _bufs(out))

    def dma(self, out, in_):
        self.P.emit(self.P.sp, lambda h: h.dma_start(out=out.ap, in_=in_.ap), _bufs(in_), _bufs(out), dma=True)

    def store(self, out, in_):
        self.P.emit(self.P.pool, lambda h: h.dma_start(out=out.ap, in_=in_.ap), _bufs(in_), _bufs(out), dma=True)

    def memset(self, eng, out, val):
        self.P.emit(self.E(eng), lambda h: h.memset(out.ap, val), [], _bufs(out))

    def scan(self, out, d0, d1):
        self.P.emit(self.P.dve, lambda h: h.tensor_tensor_scan(out=out.ap, data0=d0.ap, data1=d1.ap, initial=0.0,
                                                               op0=ALU.mult, op1=ALU.add),
                    _bufs(d0, d1), _bufs(out))

    def recip(self, out, in_):
        self.P.emit(self.P.dve, lambda h: h.reciprocal(out=out.ap, in_=in_.ap), _bufs(in_), _bufs(out))

    def reduce(self, out, in_, op=ALU.add):
        self.P.emit(self.P.dve, lambda h: h.tensor_reduce(out=out.ap, in_=in_.ap, axis=AX.X, op=op),
                    _bufs(in_), _bufs(out))

    def mmx(self, out, lhsT, rhs, start, stop):
        self.P.emit(self.P.pe, lambda h: h.matmul(out.ap, lhsT=lhsT.ap, rhs=rhs.ap, start=start, stop=stop,
                                                  skip_group_check=True),
                    _bufs(lhsT, rhs), _bufs(out))

    def sbp(self, name, shape, dt):
        t = self.st.enter_context(self.nc.sbuf_tensor("s_" + name, list(shape), dt))
        return TT(t, Buf(name))

    def sb4(self, name, shape, dt):
        t = self.ph.enter_context(self.nc.sbuf_tensor(f"s{getattr(self, 'phase_id', 0)}_" + name, list(shape), dt))
        return TTS(t, name)

    def rr(self, engs):
        self.n += 1
        return engs[self.n % len(engs)]


LD = 0.5 * float(np.exp(-0.5))
P_MUP, P_MUN, P_W0, P_A0, P_KK, P_KA, P_RK, P_LG, P_LB, P_BM, NPP = 0, 14, 28, 36, 44, 48, 52, 56, 60, 64, 65
D_C0, D_HW0, D_HA0, D_HK, D_OMHK = 0, 14, 22, 30, 34


def build_program(ntok=NTOK, debug=(), phases=(0, 1, 2, 3, 4, 5), feed=()):
    nc = bass.Bass("TRN2", target_bir_lowering=False)
    ntile = ntok // TT_
    with ExitStack() as st:
        k = K(nc, st, debug, feed)
        NCH = ntok // 128
        x_in = k.inp("x", [ntok, D])
        w1u_in = k.inp("w1u", [D, 2 * FF])
        w1d_in = k.inp("w1d", [FF, D])
        w2u_in = k.inp("w2u", [D, 2 * FF])
        w2d_in = k.inp("w2d", [FF, D])
        win_in = k.inp("win", [D, NEXT])
        wa_in = k.inp("wa", [512, D])
        wb_in = k.inp("wb", [512, D])
        wo_in = k.inp("wo", [D, D])
        lnp_in = k.inp("lnp", [6, D])
        ident_in = k.inp("ident", [128, 128])
        cos_in = k.inp("cosT", [128, ntok])
        sin_in = k.inp("sinT", [128, ntok])
        pp_in = k.inp("pp", [128, NPP])
        w2_in = k.inp("w2", [64, 2, 512])
        a2_in = k.inp("a2", [64, 2, 512])
        g2_in = k.inp("g2", [128, 512])
        rmask_in = k.inp("rmask", [128, 512])
        bones_in = k.inp("bones", [128, 128])
        hsel_in = k.inp("hsel", [2, 128, 4, 128])
        mask1_in = k.inp("mask1", [2, 128, 512])
        maska_in = k.inp("maska", [2, 128, 512])
        rint_in = k.inp("rint", [4, 128, 128])
        rdq_in = k.inp("rdq", [2, 4, 128, 512])
        rkd_in = k.inp("rkd", [128, 8])
        y_out = k.outp("y", [ntok, D])

        W1U = k.dram("W1U", [2 * FF // 512, 128, 8, 512], BF16)
        W1D = k.dram("W1D", [128, NJ, D], BF16)
        W2U = k.dram("W2U", [2 * FF // 512, 128, 8, 512], BF16)
        W2D = k.dram("W2D", [128, NJ, D], BF16)
        WIN = k.dram("WIN", [(NEXT + 511) // 512, 128, 8, 512], BF16)
        WAs = k.dram("WAs", [128, 4, D], BF16)
        WBs = k.dram("WBs", [128, 4, D], BF16)
        WOs = k.dram("WOs", [128, 8, D], BF16)
        X1 = k.dram("X1", [ntok, D], F32)
        X2 = k.dram("X2", [ntok, D], F32)
        UA = k.dram("UA", [1792, ntok], BF16)
        QR = k.dram("QR", [512, ntok], BF16)
        KR = k.dram("KR", [512, ntok], BF16)
        SG = k.dram("SG", [512, ntok], BF16)
        TG = k.dram("TG", [2048, ntok], BF16)
        VR = k.dram("VR", [ntok, 512], BF16)
        AR = k.dram("AR", [2, 4, 128, NCH, 256], BF16)
        BK = k.dram("BK", [2, 4, 128, NCH, 256], BF16)
        BKT = k.dram("BKT", [2, ntok, 2, 512], BF16)
        VT = k.dram("VT", [ntok, 512], BF16)
        GG = k.dram("GG", [512, ntok], F32)
        BVG = k.dram("BVG", [512, ntok], F32)
        YD = [k.dram("YF", [ntok, 512], F32), k.dram("YB", [ntok, 512], F32)]
        OD = k.dram("OD", [ntok, 512], F32)
        bW = {n: Buf(n) for n in ("W1U", "W1D", "W2U", "W2D", "WIN", "WAs", "WBs", "WOs")}
        tb = lambda n: [Buf(f"{n}_{i}") for i in range(ntile)]
        bX1, bX2, bUA, bQR, bKR, bSG, bTG, bVR = (tb(n) for n in ("X1", "X2", "UA", "QR", "KR", "SG", "TG", "VR"))
        bAR = [tb(f"AR{d}") for d in range(2)]
        bBK = [tb(f"BK{d}") for d in range(2)]
        bBKT = [tb(f"BKT{d}") for d in range(2)]
        bVT, bGG, bBVG, bOD = tb("VT"), tb("GG"), tb("BVG"), tb("OD")
        bYD = [tb(f"YD{d}") for d in range(2)]

        ps = [k.psum(f"ps{i}") for i in range(8)]

        idf = k.sbp("idf", [128, 128], F32)
        idb = k.sbp("idb", [128, 128], BF16)
        k.dma(idf[:], V(ident_in, []))
        k.cp("dve", idb[:], idf[:])
        etot = k.sbp("etot", [128, 2, 4, NCH], F32)
        ppt = k.sbp("ppt", [128, NPP], F32)
        k.dma(ppt[:], V(pp_in, []))

        late_rounds = []
        if 0 in phases:
            k.new_phase()
            NSTG = 5
            stg = [k.sb(f"stg{i}", [128, 4096], F32) for i in range(NSTG)]
            stb = [k.sb(f"stb{i}", [128, 4096], BF16) for i in range(NSTG)]
            rnd = [0]

            rounds = []

            def cast_round(src_ap, dst_ap, shape, dbuf):
                rounds.append((src_ap, dst_ap, shape, dbuf))

            def emit_rounds():
                early = [r for r in rounds if r[3] in (bW["W1U"], bW["W1D"], bW["WIN"])]
                late_rounds.extend(r for r in rounds if r[3] not in (bW["W1U"], bW["W1D"], bW["WIN"]))
                rounds[:] = early

                def views(r):
                    i = r % NSTG
                    a, b = rounds[r][2]
                    sv = stg[i].t[:, 0:a * b].rearrange("p (a b) -> p a b", a=a)
                    bv = stb[i].t[:, 0:a * b].rearrange("p (a b) -> p a b", a=a)
                    return i, sv, bv
                LA = NSTG - 1
                for r in range(min(LA, len(rounds))):
                    i, sv, bv = views(r)
                    k.dma(V(sv, stg[i].b), V(rounds[r][0], []))
                for r in range(len(rounds)):
                    i, sv, bv = views(r)
                    k.cp(["dve", "act"][r % 2], V(bv, stb[i].b), V(sv, stg[i].b))
                    k.store(V(rounds[r][1], rounds[r][3]), V(bv, stb[i].b))
                    if r + LA < len(rounds):
                        i2, sv2, bv2 = views(r + LA)
                        k.dma(V(sv2, stg[i2].b), V(rounds[r + LA][0], []))

            def cast_kn(src, dst, ncols, dbuf):
                v = src.rearrange("(k p) n -> p k n", p=128)
                for si, c0 in enumerate(range(0, ncols, 512)):
                    w = min(512, ncols - c0)
                    cast_round(v[:, :, c0:c0 + w], dst[si, :, :, 0:w], (8, w), dbuf)

            def cast_jn(src, dst, dbuf):
                v = src.rearrange("(j p) n -> p j n", p=128)
                nj_ = v.shape[1]
                for j0 in range(0, nj_, 4):
                    nj = min(4, nj_ - j0)
                    cast_round(v[:, j0:j0 + nj, :], dst[:, j0:j0 + nj, :], (nj, D), dbuf)

            cast_kn(w1u_in, W1U, 2 * FF, bW["W1U"])
            cast_jn(w1d_in, W1D, bW["W1D"])
            cast_kn(win_in, WIN, NEXT, bW["WIN"])
            cast_jn(wa_in, WAs, bW["WAs"])
            cast_jn(wb_in, WBs, bW["WBs"])
            cast_jn(wo_in, WOs, bW["WOs"])
            cast_kn(w2u_in, W2U, 2 * FF, bW["W2U"])
            cast_jn(w2d_in, W2D, bW["W2D"])
            emit_rounds()

        def transpose_to(srcb, dstT, pbank=0):
            for s in range(4):
                pb = ps[pbank + (s % 2)]
                pv = pb.t[:].bitcast(BF16)
                for kk in range(8):
                    k.tr(V(pv[:, kk * 128:(kk + 1) * 128], pb.b), srcb[:, s, kk * 128:(kk + 1) * 128], idb[:])
                k.cp("act" if s % 2 else "dve", dstT[:, :, s * 128:(s + 1) * 128],
                     V(pv.rearrange("p (a b) -> p a b", a=8), pb.b))

        def layer_norm_tiles(zt, xo, xob, g_bc, b_bc, stats, mv, rstd, nmr, eps, geng="dve"):
            for s in range(4):
                k.P.emit(k.P.dve, lambda h, s=s: h.bn_aggr(out=mv.t[:, s, :],
                                                           in_=stats.t[:, s, :, :].rearrange("p a b -> p (a b)")),
                         [stats.b], [mv.b])
            k.ts("dve", rstd[:], mv[:, :, 1], eps, ALU.add)
            k.act(rstd[:], rstd[:], AF.Sqrt)
            k.recip(rstd[:], rstd[:])
            k.stt(nmr[:], mv[:, :, 0], -1.0, rstd[:], ALU.mult, ALU.mult)
            for s in range(4):
                k.act(zt[:, s, :], zt[:, s, :], AF.Identity, scale=rstd[:, s:s + 1], bias=nmr[:, s:s + 1])
                k.tt(geng, zt[:, s, :], zt[:, s, :], g_bc[:], ALU.mult)
                if xob is not None:
                    k.tt("dve", xob[:, s, :], zt[:, s, :], b_bc[:], ALU.add)
                k.tt("pool", xo[:, s, :], zt[:, s, :], b_bc[:], ALU.add)

        def bn_stats(stats, s, hf, zt):
            k.P.emit(k.P.dve, lambda h: h.bn_stats(out=stats.t[:, s, hf, :], in_=zt.t[:, s, hf * 512:(hf + 1) * 512]),
                     zt[:, s, 0:1].bs, [stats.b])

        def ffn_phase(tag, src, src_bufs, WU, bWU, WD, bWD, lrow, post, alloc_extra, want_bf=True):
            k.new_phase()
            wd = k.sb("wd", [128, NJ, D], BF16)
            g_bc = k.sb("g_bc", [128, D], F32)
            b_bc = k.sb("b_bc", [128, D], F32)

            def load_resident():
                k.dma(wd[:], V(WD, bWD))
                k.dma(g_bc[:], V(lnp_in[lrow:lrow + 1, :].partition_broadcast(128), []))
                k.dma(b_bc[:], V(lnp_in[lrow + 1:lrow + 2, :].partition_broadcast(128), []))
            xtok = [k.sb4(f"xtok{i}", [128, 4, D], F32) for i in range(3)]
            xbf = k.sb("xbf", [128, 4, D], BF16)
            x1b = k.sb4("x1b", [128, 4, D], BF16)
            xT = k.sb("xT", [128, 8, TT_], BF16)
            actT = [k.sb(f"actT{j}", [128, TT_], BF16) for j in range(NJ)]
            wu = [k.sb(f"wu{i}", [128, 8, 512], BF16) for i in range(3)]
            th = [k.sb(f"th{i}", [128, TT_], F32) for i in range(2)]
            sgt = [k.sb(f"sgt{i}", [128, TT_], F32) for i in range(2)]
            stats = k.sb("stats", [128, 4, 2, 6], F32)
            mv = k.sb("mv", [128, 4, 2], F32)
            rstd = k.sb("rstd", [128, 4], F32)
            nmr = k.sb("nmr", [128, 4], F32)
            ups = [0]
            extra = alloc_extra()

            def next_w():
                w = wu[ups[0] % 3]
                ups[0] += 1
                return w

            def load_x(ti):
                t0 = ti * TT_
                k.dma(xtok[ti % 3][:], V(src[t0:t0 + TT_, :].rearrange("(s p) d -> p s d", p=128),
                                         [src_bufs[ti]] if src_bufs else []))

            def stage_a0(ti):
                xt = xtok[ti % 3]
                for s in range(4):
                    k.cp("act" if s % 2 else "dve", xbf[:, s, :], xt[:, s, :])

                transpose_to(xbf, xT)

            def stage_a(ti):
                xt = xtok[ti % 3]
                for s in range(NJ // 2):
                    w = next_w()
                    k.dma(w[:], V(WU[s], bWU))
                    if ti == 0 and s == 1:
                        load_resident()
                    for jj in range(2):
                        j = 2 * s + jj
                        pg, pu = ps[2 + 2 * (j % 2)], ps[3 + 2 * (j % 2)]
                        for kk in range(8):
                            k.mm(pg[:], w[:, kk, jj * 128:(jj + 1) * 128], xT[:, kk, :], kk == 0, kk == 7)
                        for kk in range(8):
                            k.mm(pu[:], w[:, kk, 256 + jj * 128:256 + (jj + 1) * 128], xT[:, kk, :], kk == 0, kk == 7)
                        k.act(th[j % 2][:], pg[:], AF.Tanh, scale=0.5)
                        k.stt(sgt[j % 2][:], th[j % 2][:], 1.0, pg[:], ALU.add, ALU.mult)
                        k.tt("dve", actT[j][:], sgt[j % 2][:], pu[:], ALU.mult)
                for s in range(4):
                    for hf in range(2):
                        pz = ps[6 + (2 * s + hf) % 2]
                        for j in range(NJ):
                            k.mm(pz[:], actT[j][:, s * 128:(s + 1) * 128], wd[:, j, hf * 512:(hf + 1) * 512],
                                 j == 0, j == NJ - 1)
                        k.stt(xt[:, s, hf * 512:(hf + 1) * 512], xt[:, s, hf * 512:(hf + 1) * 512], 4.0 * ALPHA,
                              pz[:], ALU.mult, ALU.add)
                        bn_stats(stats, s, hf, xt)

            def stage_l(ti):
                xt = xtok[ti % 3]
                layer_norm_tiles(xt, xt, x1b if want_bf else None, g_bc, b_bc, stats, mv, rstd, nmr, 16.0 * LN_EPS)

            def stage_b(ti):
                post(ti, xtok[ti % 3], x1b, xT, next_w, th, sgt, extra)

            load_x(0)
            stage_a0(0)
            stage_a(0)
            for ti in range(1, min(3, ntile)):
                load_x(ti)
            if ntile > 1:
                stage_a0(1)
            stage_l(0)
            for ti in range(1, ntile):
                stage_a(ti)
                stage_b(ti - 1)
                if ti + 2 < ntile:
                    load_x(ti + 2)
                if ti + 1 < ntile:
                    stage_a0(ti + 1)
                stage_l(ti)
            stage_b(ntile - 1)

        def p1_alloc():
            e = {}
            e["cs"] = [k.sb(f"cs{i}", [128, TT_], F32) for i in range(2)]
            e["sn"] = [k.sb(f"sn{i}", [128, TT_], F32) for i in range(2)]
            e["ost"] = [k.sb(f"ost{i}", [128, 4, TT_], BF16) for i in range(3)]
            e["ostn"] = [0]
            return e

        def p1_post(ti, x1, x1b, x1T, next_w, th, sgt, e):
            t0 = ti * TT_
            cs, sn, ost, ostn = e["cs"], e["sn"], e["ost"], e["ostn"]
            t1, t2 = th, sgt
            k.store(V(X1[t0:t0 + TT_, :].rearrange("(s p) d -> p s d", p=128), bX1[ti]), x1[:])
            transpose_to(x1b, x1T)
            k.dma(cs[ti % 2][:], V(cos_in[:, t0:t0 + TT_], []))
            k.dma(sn[ti % 2][:], V(sin_in[:, t0:t0 + TT_], []))
            pcnt = [0]

            def proj_chunk(w, cl):
                pb = ps[2 + pcnt[0] % 6]
                pcnt[0] += 1
                for kk in range(8):
                    k.mm(pb[:], w[:, kk, cl * 128:(cl + 1) * 128], x1T[:, kk, :], kk == 0, kk == 7)
                return pb

            def flush(o, dst, row0, nrows, dbuf):
                k.store(V(dst[row0:row0 + nrows * 128, t0:t0 + TT_].rearrange("(a p) t -> p a t", p=128), dbuf),
                      o[:, 0:nrows, :])

            def nost():
                o = ost[ostn[0] % 3]
                ostn[0] += 1
                return o

            cur_sl = [-1]
            wcur = [None]

            slabs = {}

            def load_slab(sl):
                if sl in slabs or sl > 12:
                    return
                w = next_w()
                k.dma(w[:], V(WIN[sl], bW["WIN"]))
                slabs[sl] = w

            def get_chunk(c):
                sl = c // 4
                if sl != cur_sl[0]:
                    load_slab(sl)
                    load_slab(sl + 1)
                    cur_sl[0] = sl
                    wcur[0] = slabs[sl]
                return proj_chunk(wcur[0], c % 4)

            c = 0
            for (row0, ncl) in ((0, 4), (512, 4), (1024, 4), (1536, 2)):
                o = nost()
                for cl in range(ncl):
                    pb = get_chunk(c)
                    c += 1
                    k.cp("act" if cl % 2 else "dve", o[:, cl, :], pb[:])
                flush(o, UA, row0, ncl, bUA[ti])
            for (dst, dbuf) in ((QR, bQR[ti]), (KR, bKR[ti])):
                o = nost()
                for hh in range(4):
                    pq = get_chunk(c)
                    pqs = get_chunk(c + 1)
                    c += 2
                    k.tt("dve", t1[hh % 2][:], pq[:], cs[ti % 2][:], ALU.mult)
                    k.tt("dve", t2[hh % 2][:], pqs[:], sn[ti % 2][:], ALU.mult)
                    k.tt("pool", o[:, hh, :], t1[hh % 2][:], t2[hh % 2][:], ALU.add)
                flush(o, dst, 0, 4, dbuf)
            o = nost()
            for hh in range(4):
                pg = get_chunk(c)
                c += 1
                k.act(th[hh % 2][:], pg[:], AF.Tanh, scale=0.5)
                k.stt(o[:, hh, :], th[hh % 2][:], 1.0, pg[:], ALU.add, ALU.mult)
            flush(o, SG, 0, 4, bSG[ti])
            for gg in range(4):
                o = nost()
                for hh in range(4):
                    pg = get_chunk(c)
                    c += 1
                    k.act(o[:, hh, :], pg[:], AF.Tanh, scale=0.5)
                flush(o, TG, gg * 512, 4, bTG[ti])
            assert c == 50
            load_slab(12)
            w = slabs[12]
            w2_ = next_w()
            k.dma(w2_[:, :, 0:256], V(WIN[13, :, :, 0:256], bW["WIN"]))
            o = nost()
            for s in range(4):
                pb = ps[2 + pcnt[0] % 6]
                pcnt[0] += 1
                for kk in range(8):
                    k.mm(pb[:, 0:256], x1T[:, kk, s * 128:(s + 1) * 128], w[:, kk, 256:512], kk == 0, kk == 7)
                for kk in range(8):
                    k.mm(pb[:, 256:512], x1T[:, kk, s * 128:(s + 1) * 128], w2_[:, kk, 0:256], kk == 0, kk == 7)
                k.cp("act" if s % 2 else "dve", o[:, s, :], pb[:])
            k.store(V(VR[t0:t0 + TT_, :].rearrange("(s p) c -> p s c", p=128), bVR[ti]), o[:])

        if 1 in phases:
            ffn_phase("f1", x_in, None, W1U, bW["W1U"], W1D, bW["W1D"], 0, p1_post, p1_alloc)

        if 2 in phases:
            k.new_phase()
            dp = k.sb("dp", [128, 40], F32)
            k.tt("dve", dp[:, 0:14], ppt[:, P_MUP:P_MUP + 14], ppt[:, P_MUN:P_MUN + 14], ALU.add)
            k.ts("dve", dp[:, 0:14], dp[:, 0:14], -1.0, ALU.mult, 1.0, ALU.add)
            k.ts("dve", dp[:, D_HW0:D_HW0 + 8], ppt[:, P_W0:P_W0 + 8], 0.5, ALU.mult)
            k.ts("dve", dp[:, D_HA0:D_HA0 + 8], ppt[:, P_A0:P_A0 + 8], 0.5, ALU.mult)
            k.ts("dve", dp[:, D_HK:D_HK + 4], ppt[:, P_KA:P_KA + 4], 0.5, ALU.mult)
            k.ts("dve", dp[:, D_OMHK:D_OMHK + 4], ppt[:, P_KA:P_KA + 4], -0.5, ALU.mult, 1.0, ALU.add)
            stgs = k.sb("stgs", [128, 1024], F32)
            stgs2 = k.sb("stgs2", [128, 512], F32)
            stgs3 = k.sb("stgs3", [128, 512], F32)
            w2b = k.sb("w2b", [64, 2, 512], BF16)
            a2b = k.sb("a2b", [128, 2, 512], BF16)
            g2b = k.sb("g2b", [128, 512], BF16)
            bones = k.sb("bones", [128, 128], BF16)
            hsel = k.sb("hsel", [128, 4, 128], BF16)
            hselT = k.sb("hselT", [128, 4, 128], F32)
            rmask = k.sb("rmask", [128, 512], F32)
            k.dma(rmask[:], V(rmask_in, []))
            k.dma(stgs[0:64, :], V(w2_in.rearrange("p d n -> p (d n)"), []))
            k.cp("dve", V(w2b.t[:].rearrange("p d n -> p (d n)"), w2b.b), stgs[0:64, :])
            k.dma(stgs[64:128, :], V(a2_in.rearrange("p d n -> p (d n)"), []))
            k.cp("dve", V(a2b.t[64:128].rearrange("p d n -> p (d n)"), a2b.b), stgs[64:128, :])
            k.dma(stgs2[:, 0:512], V(g2_in, []))
            k.cp("act", g2b[:], stgs2[:, 0:512])
            k.dma(stgs3[:, 0:128], V(bones_in, []))
            k.cp("dve", bones[:], stgs3[:, 0:128])
            k.dma(stgs2[:, 0:512], V(hsel_in[0].rearrange("p c r -> p (c r)"), []))
            k.cp("act", V(hsel.t[:].rearrange("p c r -> p (c r)"), hsel.b), stgs2[:, 0:512])
            k.dma(V(hselT.t[:].rearrange("p c r -> p (c r)"), hselT.b), V(hsel_in[1].rearrange("p c r -> p (c r)"), []))

            uh = [k.sb(f"uh{i}", [128, 14, 514], BF16) for i in range(2)]
            S = [k.sb(f"S{c}", [128, 512], F32) for c in range(14)]
            tx = k.sb("tx", [128, 512], BF16)
            xab = k.sb("xab", [128, 512], BF16)
            sigxf = k.sb("sigxf", [128, 512], F32)
            sigx = k.sb("sigx", [128, 512], BF16)
            kk2 = [k.sb(f"kk2_{i}", [128, 512], BF16) for i in range(2)]
            rn = k.sb("rn", [128, 512], F32)
            kkn = [k.sb(f"kkn{c}", [128, 512], F32) for c in range(4)]
            vsb = [k.sb(f"vsb{c}", [128, 512], BF16) for c in range(4)]
            NB = 3
            ones1 = k.sb("ones1", [128, 1], F32)
            k.memset("pool", ones1[:], 1.0)
            tw = [k.sb(f"tw{i}", [128, 512], F32) for i in range(NB)]
            tw1 = [k.sb(f"tw1_{i}", [128, 512], F32) for i in range(NB)]
            cum = [k.sb(f"cum{i}", [128, 512], F32) for i in range(NB)]
            cumx = [k.sb(f"cumx{i}", [128, 512], F32) for i in range(NB)]
            E1 = [k.sb(f"E1_{i}", [128, 512], F32) for i in range(2)]
            E2 = [k.sb(f"E2_{i}", [128, 512], F32) for i in range(2)]
            E3 = [k.sb(f"E3_{i}", [128, 512], F32) for i in range(2)]
            ta = [k.sb(f"ta{i}", [128, 512], F32) for i in range(NB)]
            ff_ = [k.sb(f"ff{i}", [128, 512], F32) for i in range(NB)]
            b2 = [k.sb(f"b2_{i}", [128, 512], F32) for i in range(NB)]
            kdir = [[k.sb(f"kdir{p}_{d}", [128, 512], F32) for d in range(2)] for p in range(2)]
            ARt = [k.sb(f"ARt{i}", [128, 4, 2, 128], BF16) for i in range(NB)]
            BKt = [k.sb(f"BKt{i}", [128, 4, 2, 128], BF16) for i in range(NB)]
            BKTt = [k.sb(f"BKTt{d}", [128, 4, 2, 512], BF16) for d in range(2)]
            VTt = k.sb("VTt", [128, 4, 512], BF16)
            kds = k.sb("kds", [128, 512], F32)
            rb = k.sb("rb", [128, 512], BF16)
            bv = k.sb("bv", [128, 512], F32)
            ggt = [k.sb(f"ggt{i}", [128, 512], F32) for i in range(2)]
            bvgt = [k.sb(f"bvgt{i}", [128, 512], F32) for i in range(2)]
            pc = [0]

            def nps():
                pc[0] += 1
                return ps[pc[0] % 8]

            def v4(t):
                return V(t.t[:].rearrange("p (n x) -> p n x", n=4), t.b)

            def load_u(ti):
                t0 = ti * TT_
                U = uh[ti % 2]
                lo, hi = max(t0 - 1, 0), min(t0 + 513, ntok)
                bl = [bUA[j] for j in range(max(ti - 1, 0), min(ti + 2, ntile))]
                k.dma(U[:, :, lo - (t0 - 1):hi - (t0 - 1)], V(UA[:, lo:hi].rearrange("(c p) t -> p c t", p=128), bl))
                if t0 == 0:
                    k.memset("pool", U[:, :, 0:1], 0.0)
                if t0 + TT_ == ntok:
                    k.memset("pool", U[:, :, 513:514], 0.0)
                if t0 == ntok // 2:
                    k.ts("pool", U[:, :, 0:1], U[:, :, 0:1], ppt[:, P_BM:P_BM + 1], ALU.mult)
                if t0 + TT_ == ntok // 2:
                    k.ts("pool", U[:, :, 513:514], U[:, :, 513:514], ppt[:, P_BM:P_BM + 1], ALU.mult)

            load_u(0)
            for ti in range(ntile):
                t0 = ti * TT_
                U = uh[ti % 2]
                if ti + 1 < ntile:
                    load_u(ti + 1)
                def shift(tj, cs_):
                    Uj = uh[tj % 2]
                    for c in cs_:
                        k.act(S[c][:], Uj[:, c, 1:513], AF.Identity, scale=dp[:, D_C0 + c:D_C0 + c + 1])
                        k.stt(S[c][:], Uj[:, c, 0:512], ppt[:, P_MUP + c:P_MUP + c + 1], S[c][:], ALU.mult, ALU.add)
                        k.stt(S[c][:], Uj[:, c, 2:514], ppt[:, P_MUN + c:P_MUN + c + 1], S[c][:], ALU.mult, ALU.add)

                if ti == 0:
                    shift(0, range(14))
                    for c in range(4):
                        k.cp("pool", vsb[c][:], S[8 + c][:])
                k.act(tx[0:64, :], S[12][0:64, :], AF.Tanh)
                k.cp("pool", xab[64:128, :], S[12][64:128, :])
                k.act(sigxf[:], S[13][:], AF.Tanh, scale=0.5)
                k.ts("pool", sigx[:], sigxf[:], 0.5, ALU.mult, 0.5, ALU.add)
                pn = nps()
                for c in range(4):
                    k.act(kk2[c % 2][:], S[4 + c][:], AF.Square, scale=ppt[:, P_KK + c:P_KK + c + 1])
                    k.mm(pn[:], hsel[:, c, :], kk2[c % 2][:], c == 0, c == 3)
                k.ts("dve", rn[:], pn[:], 1e-24, ALU.max)
                k.act(rn[:], rn[:], AF.Sqrt)
                k.recip(rn[:], rn[:])
                for c in range(4):
                    pr = nps()
                    k.mm(pr[:], hselT[:, c, :], rn[:])
                    k.stt(kkn[c][:], S[4 + c][:], ppt[:, P_KK + c:P_KK + c + 1], pr[:], ALU.mult, ALU.mult)
                def h1(j):
                    c, d = j // 2, j % 2
                    i_ = j % NB
                    pz = nps()
                    k.mm(pz[:], w2b[0:64, d, c * 128:(c + 1) * 128], tx[0:64, :])
                    k.act(tw[i_][:], pz[:], AF.Tanh, scale=0.5, bias=dp[:, D_HW0 + d * 4 + c:D_HW0 + d * 4 + c + 1])
                    k.act(tw1[i_][:], tw[i_][:], AF.Identity, bias=ones1[:, 0:1])
                    k.scan(cum[i_][:], rmask[:], tw1[i_][:])
                    k.tt("pool", cumx[i_][:], cum[i_][:], tw1[i_][:], ALU.subtract)
                    pa = nps()
                    k.mm(pa[:], a2b[64:128, d, c * 128:(c + 1) * 128], xab[64:128, :])
                    k.act(ta[i_][:], pa[:], AF.Tanh, scale=0.5, bias=dp[:, D_HA0 + d * 4 + c:D_HA0 + d * 4 + c + 1])
                    k.act(ff_[i_][:], ta[i_][:], AF.Identity, scale=dp[:, D_HK + c:D_HK + c + 1],
                          bias=dp[:, D_OMHK + c:D_OMHK + c + 1])
                    k.tt("pool", kdir[c % 2][d][:], ff_[i_][:], S[4 + c][:], ALU.mult)
                    k.stt(b2[i_][:], ta[i_][:], 1.0, kkn[c][:], ALU.add, ALU.mult)

                def h2(j):
                    c, d = j // 2, j % 2
                    i_ = j % NB
                    if d == 0:
                        k.act(E1[j % 2][:], cum[i_][:], AF.Exp, scale=-LD)
                        k.act(E2[j % 2][:], cum[i_][:], AF.Exp, scale=LD)
                        k.act(E3[j % 2][:], cumx[i_][:], AF.Exp, scale=-LD)
                    else:
                        k.act(E1[j % 2][:], cumx[i_][:], AF.Exp, scale=LD)
                        k.act(E2[j % 2][:], cumx[i_][:], AF.Exp, scale=-LD)
                        k.act(E3[j % 2][:], cum[i_][:], AF.Exp, scale=LD)
                    k.act(etot[:, d, c, ti * 4:(ti + 1) * 4],
                          V(cum[i_].t[:].rearrange("p (n x) -> p n x", n=4)[:, :, 127], cum[i_].b), AF.Exp, scale=-LD)
                    At, Bt = ARt[i_], BKt[i_]
                    k.stt(At[:, :, 0, :], v4(kkn[c]), -1.0, v4(E3[j % 2]), ALU.mult, ALU.mult)
                    k.tt("pool", At[:, :, 1, :], v4(S[c]), v4(E1[j % 2]), ALU.mult)
                    k.stt(Bt[:, :, 0, :], v4(b2[i_]), 0.5, v4(E2[j % 2]), ALU.mult, ALU.mult)
                    k.tt("pool", Bt[:, :, 1, :], v4(kdir[c % 2][d]), v4(E2[j % 2]), ALU.mult)
                    k.dma(V(AR[d, c, :, ti * 4:(ti + 1) * 4, :].rearrange("p n (j x) -> p n j x", j=2), bAR[d][ti]), At[:])
                    k.dma(V(BK[d, c, :, ti * 4:(ti + 1) * 4, :].rearrange("p n (j x) -> p n j x", j=2), bBK[d][ti]), Bt[:])
                    pt = nps()
                    pv = pt.t[:].bitcast(BF16)
                    for n in range(4):
                        for jj in range(2):
                            k.tr(V(pv[:, (n * 2 + jj) * 128:(n * 2 + jj + 1) * 128], pt.b), Bt[:, n, jj, :], idb[:])
                    k.cp("act", BKTt[d][:, :, :, c * 128:(c + 1) * 128],
                         V(pv.rearrange("p (n j x) -> p n j x", n=4, j=2), pt.b))

                def tail(c):
                    k.tt("pool", kds[:], kdir[c % 2][0][:], kdir[c % 2][1][:], ALU.add)
                    k.stt(rb[:], S[c][:], ppt[:, P_RK + c:P_RK + c + 1], kds[:], ALU.mult, ALU.mult)
                    pb_ = nps()
                    k.mm(pb_[:], bones[:], rb[:])
                    k.tt("dve", bv[:], pb_[:], S[8 + c][:], ALU.mult)
                    pg = nps()
                    k.mm(pg[:], g2b[:, c * 128:(c + 1) * 128], sigx[:])
                    k.act(ggt[c % 2][:], pg[:], AF.Identity, scale=ppt[:, P_LG + c:P_LG + c + 1])
                    k.stt(bvgt[c % 2][:], bv[:], ppt[:, P_LB + c:P_LB + c + 1], pg[:], ALU.add, ALU.mult)
                    k.dma(V(GG[c * 128:(c + 1) * 128, t0:t0 + TT_], bGG[ti]), ggt[c % 2][:])
                    k.dma(V(BVG[c * 128:(c + 1) * 128, t0:t0 + TT_], bBVG[ti]), bvgt[c % 2][:])
                    pt = nps()
                    pv = pt.t[:].bitcast(BF16)
                    for n in range(4):
                        k.tr(V(pv[:, n * 128:(n + 1) * 128], pt.b), vsb[c][:, n * 128:(n + 1) * 128], idb[:])
                    k.cp("act", VTt[:, :, c * 128:(c + 1) * 128], V(pv[:, 0:512].rearrange("p (n x) -> p n x", n=4), pt.b))

                def next_shift(c):
                    if ti + 1 < ntile:
                        shift(ti + 1, (c, 4 + c, 8 + c))
                        k.cp("pool", vsb[c][:], S[8 + c][:])

                h1(0)
                for j in range(1, 8):
                    h1(j)
                    h2(j - 1)
                    if (j - 1) % 2 == 1:
                        tail((j - 1) // 2)
                        next_shift((j - 1) // 2)
                h2(7)
                tail(3)
                next_shift(3)
                if ti + 1 < ntile:
                    shift(ti + 1, (12, 13))
                for d in range(2):
                    k.dma(V(BKT[d, t0:t0 + TT_, :, :].rearrange("(n p) j x -> p n j x", p=128), bBKT[d][ti]), BKTt[d][:])
                k.dma(V(VT[t0:t0 + TT_, :].rearrange("(n p) x -> p n x", p=128), bVT[ti]), VTt[:])
            k.ts("dve", etot[:, :, :, NCH // 2 - 1], etot[:, :, :, NCH // 2 - 1], ppt[:, P_BM:P_BM + 1], ALU.mult)

        if 2 in phases:
            k.new_phase()
            stgm = k.sb("stgm", [128, 512], F32)
            mask1 = [k.sb(f"mask1_{d}", [128, 512], BF16) for d in range(2)]
            maska = [k.sb(f"maska_{d}", [128, 512], BF16) for d in range(2)]
            for d in range(2):
                k.dma(stgm[:], V(mask1_in[d], []))
                k.cp("dve", mask1[d][:], stgm[:])
                k.dma(stgm[:], V(maska_in[d], []))
                k.cp("dve", maska[d][:], stgm[:])
            SBDf = [k.sb(f"SBDf{d}", [128, 4, 128], F32) for d in range(2)]
            SBDb = [[k.sb(f"SBDb{p}_{d}", [128, 4, 128], BF16) for d in range(2)] for p in range(2)]
            for d in range(2):
                k.memset("pool", SBDf[d][:], 0.0)
                for p in range(2):
                    k.memset("pool", SBDb[p][d][:], 0.0)
            ARs = [[k.sb(f"ARs{p}_{d}", [128, 4, 256], BF16) for d in range(2)] for p in range(2)]
            BKs = [[k.sb(f"BKs{p}_{d}", [128, 4, 256], BF16) for d in range(2)] for p in range(2)]
            BKTs = [[k.sb(f"BKTs{p}_{d}", [128, 2, 512], BF16) for d in range(2)] for p in range(2)]
            VTs = [[k.sb(f"VTs{p}_{d}", [128, 512], BF16) for d in range(2)] for p in range(2)]
            M1 = [[[k.sb(f"M1_{p}_{d}_{h}", [128, 512], BF16) for h in range(8)] for d in range(2)] for p in range(2)]
            A0 = [[k.sb(f"A0_{d}_{hp}", [128, 512], BF16) for hp in range(2)] for d in range(2)]
            CB = [[[k.sb(f"CB{l}_{d}_{q}", [128, 512], BF16) for q in range(4)] for d in range(2)] for l in range(2)]
            Tb = [[[[k.sb(f"Tb{p}_{l}_{d}_{q}", [128, 256], BF16) for q in range(4)] for d in range(2)]
                   for l in range(2)] for p in range(2)]
            Xb = [k.sb(f"Xb{d}", [128, 512], BF16) for d in range(2)]
            m1tmp = [k.sb(f"m1tmp{i}", [128, 512], BF16) for i in range(2)]
            Ub = [k.sb(f"Ub{d}", [128, 512], BF16) for d in range(2)]
            Gt = [k.sb(f"Gt{d}", [128, 4, 128], F32) for d in range(2)]
            Ysb = [[k.sb(f"Ysb{p}_{d}", [128, 512], F32) for d in range(2)] for p in range(2)]

            def chunk_of(i, d):
                return i if d == 0 else NCH - 1 - i

            def pre_segments(i):
                par = i % 2
                segs = []

                def seg_stage1(d):
                    n = chunk_of(i, d)
                    tl = n // 4
                    k.dma(ARs[par][d][:], V(AR[d, :, :, n, :].rearrange("c p x -> p c x"), bAR[d][tl]))
                    k.dma(BKs[par][d][:], V(BK[d, :, :, n, :].rearrange("c p x -> p c x"), bBK[d][tl]))
                    k.dma(BKTs[par][d][:], V(BKT[d, n * 128:(n + 1) * 128, :, :], bBKT[d][tl]))
                    k.dma(VTs[par][d][:], V(VT[n * 128:(n + 1) * 128, :], bVT[tl]))
                    a_, b_ = ARs[par][d], BKs[par][d]
                    for h in range(8):
                        c, p0 = h // 2, 64 * (h % 2)
                        s1 = ps[h % 4]
                        k.mm(s1[:, 0:256], b_[p0:p0 + 64, c, 0:128], a_[p0:p0 + 64, c, 0:256])
                        k.mm(s1[:, 256:512], b_[p0:p0 + 64, c, 128:256], a_[p0:p0 + 64, c, 0:256])
                        if h in (3, 7):
                            k.cp("act", m1tmp[h // 4][:], s1[:])
                            k.tt("pool", M1[par][d][h][:], m1tmp[h // 4][:], mask1[d][:], ALU.mult)
                        else:
                            k.tt("dve", M1[par][d][h][:], s1[:], mask1[d][:], ALU.mult)
                    for h in range(8):
                        c, p0 = h // 2, 64 * (h % 2)
                        k.mm(ps[h % 2][:, c * 128:(c + 1) * 128], a_[p0:p0 + 64, c, 0:128], b_[p0:p0 + 64, c, 0:128])
                    for hp in range(2):
                        k.tt("dve", A0[d][hp][:], ps[hp][:], maska[d][:], ALU.mult)
                    for h in range(8):
                        q, hp = h // 2, h % 2
                        k.tt("pool", Tb[par][0][d][q][:, hp * 128:(hp + 1) * 128], M1[par][d][h][:, 0:128], idb[:], ALU.add)

                def seg_level(lv):
                    items = [(d, q) for d in range(2) for q in range(4)]

                    def chain(i):
                        d, q = items[i]
                        bank = ps[i % 4]
                        for hp in range(2):
                            h = 2 * q + hp
                            if lv == 0:
                                Ak = A0[d][hp][:, q * 128:(q + 1) * 128]
                                Bk = M1[par][d][h][:, 0:128]
                            else:
                                Ak = CB[lv % 2][d][q][:, hp * 128:(hp + 1) * 128]
                                Bk = CB[lv % 2][d][q][:, 256 + hp * 128:256 + (hp + 1) * 128]
                            k.mm(bank[:, hp * 128:(hp + 1) * 128], Bk, Ak)
                            if lv < 5:
                                k.mm(bank[:, 256 + hp * 128:256 + (hp + 1) * 128], Ak, Bk)
                        nc_ = 512 if lv < 5 else 256
                        k.cp("act", CB[(lv + 1) % 2][d][q][:, 0:nc_], bank[:, 0:nc_])

                    def tprod(i):
                        d, q = items[i]
                        bankT = ps[4 + i % 2]
                        Tc, Tn = Tb[par][lv % 2][d][q], Tb[par][(lv + 1) % 2][d][q]
                        if True:
                            for hp in range(2):
                                k.mm(bankT[:, hp * 128:(hp + 1) * 128], CB[(lv + 1) % 2][d][q][:, hp * 128:(hp + 1) * 128],
                                     Tc[:, hp * 128:(hp + 1) * 128])
                            k.tt("dve", Tn[:], bankT[:, 0:256], Tc[:], ALU.add)
                        else:
                            for hp in range(2):
                                k.mm(bankT[:, hp * 128:(hp + 1) * 128], idb[:], Tc[:, hp * 128:(hp + 1) * 128], True, False)
                                k.mm(bankT[:, hp * 128:(hp + 1) * 128], CB[(lv + 1) % 2][d][q][:, hp * 128:(hp + 1) * 128],
                                     Tc[:, hp * 128:(hp + 1) * 128], False, True)
                            k.cp("act", Tn[:], bankT[:, 0:256])

                    for i in range(8):
                        chain(i)
                        if i >= 2:
                            tprod(i - 2)
                    tprod(6)
                    tprod(7)

                segs.append(lambda: seg_stage1(0))
                segs.append(lambda: seg_stage1(1))
                for lv in range(6):
                    segs.append(lambda lv=lv: seg_level(lv))
                return segs

            def stage_X(i):
                par = i % 2
                for d in range(2):
                    px = ps[6 + d]
                    for c in range(4):
                        k.mmx(px[:, c * 128:(c + 1) * 128], ARs[par][d][:, c, 0:128], SBDb[par][d][:, c, :], True, False)
                        for hp in range(2):
                            h = 2 * c + hp
                            k.mmx(px[:, h * 64:(h + 1) * 64], M1[par][d][h][:, 256:384], VTs[par][d][:, h * 64:(h + 1) * 64],
                                  False, True)
                    k.cp("act", Xb[d][:], px[:])

            def stage_U(i):
                par = i % 2
                for d in range(2):
                    pu = ps[6 + d]
                    for h in range(8):
                        q, hp = h // 2, h % 2
                        k.mm(pu[:, h * 64:(h + 1) * 64], Tb[par][0][d][q][:, hp * 128:(hp + 1) * 128], Xb[d][:, h * 64:(h + 1) * 64])
                    k.cp("dve", Ub[d][:], pu[:])

            def stage_S(i):
                par = i % 2
                if i == NCH - 1:
                    return
                for d in range(2):
                    pd = ps[6 + d]
                    for h in range(8):
                        c, hp = h // 2, h % 2
                        o = pd[hp * 64:(hp + 1) * 64, c * 128 + hp * 64:c * 128 + (hp + 1) * 64]
                        k.mmx(o, BKTs[par][d][:, 0, h * 64:(h + 1) * 64], Ub[d][:, h * 64:(h + 1) * 64], True, False)
                        k.mmx(o, BKTs[par][d][:, 1, h * 64:(h + 1) * 64], VTs[par][d][:, h * 64:(h + 1) * 64], False, True)
                    ne = i if d == 0 else NCH - 2 - i
                    pdv = pd.t[:].rearrange("p (c x) -> p c x", c=4)
                    for hp in range(2):
                        sl = slice(hp * 64, (hp + 1) * 64)
                        ev = etot.t[sl, d, :, ne:ne + 1].to_broadcast([64, 4, 64])
                        k.tt("dve", Gt[d][sl, :, sl], SBDf[d][sl, :, sl], V(pdv[sl, :, sl], pd.b), ALU.add)
                        k.tt("dve", SBDf[d][sl, :, sl], Gt[d][sl, :, sl], V(ev, etot.b), ALU.mult)
                        k.cp("pool", SBDb[1 - par][d][sl, :, sl], SBDf[d][sl, :, sl])

            def stage_Y(i):
                par = i % 2
                for d in range(2):
                    n = chunk_of(i, d)
                    py = ps[6 + d]
                    for c in range(4):
                        k.mmx(py[:, c * 128:(c + 1) * 128], ARs[par][d][:, c, 128:256], SBDb[par][d][:, c, :], True, False)
                        for hp in range(2):
                            h = 2 * c + hp
                            k.mmx(py[:, h * 64:(h + 1) * 64], M1[par][d][h][:, 128:256], Ub[d][:, h * 64:(h + 1) * 64], False, False)
                            k.mmx(py[:, h * 64:(h + 1) * 64], M1[par][d][h][:, 384:512], VTs[par][d][:, h * 64:(h + 1) * 64],
                                  False, True)
                    k.cp("act", Ysb[par][d][:], py[:])
                    k.store(V(YD[d][n * 128:(n + 1) * 128, :], bYD[d][n // 4]), Ysb[par][d][:])

            stgL = [k.sb(f"stgL{i}", [128, 4096], F32) for i in range(2)]
            stbL = [k.sb(f"stbL{i}", [128, 4096], BF16) for i in range(2)]
            lr = [0]

            def late_cast(n):
                for _ in range(n):
                    if lr[0] >= len(late_rounds):
                        return
                    src_ap, dst_ap, (a_, b_), dbuf = late_rounds[lr[0]]
                    i_ = lr[0] % 2
                    lr[0] += 1
                    sv = stgL[i_].t[:, 0:a_ * b_].rearrange("p (a b) -> p a b", a=a_)
                    bv = stbL[i_].t[:, 0:a_ * b_].rearrange("p (a b) -> p a b", a=a_)
                    k.dma(V(sv, stgL[i_].b), V(src_ap, []))
                    k.cp("act", V(bv, stbL[i_].b), V(sv, stgL[i_].b))
                    k.store(V(dst_ap, dbuf), V(bv, stbL[i_].b))

            for sg in pre_segments(0):
                sg()
            per_step = (len(late_rounds) + NCH - 1) // NCH
            for i in range(NCH):
                segs = pre_segments(i + 1) if i + 1 < NCH else []

                def run(a, b):
                    for sg in segs[a:b]:
                        sg()
                stage_X(i)
                late_cast(per_step)
                run(0, 2)
                stage_U(i)
                run(2, 4)
                stage_S(i)
                run(4, 6)
                stage_Y(i)
                run(6, 8)
            late_cast(len(late_rounds))

        pcg = [0]

        def gps():
            pcg[0] += 1
            return ps[pcg[0] % 8]

        if 3 in phases:
            k.new_phase()
            GC = [float((1.0 - 2.0 ** (-5.0 - h)) ** 128) for h in range(4)]
            Kall = k.sb("Kall", [128, 4, ntok], BF16)
            Vall = k.sb("Vall", [128, NCH, 512], BF16)
            RfS = [k.sb(f"RfS{h}", [128, NCH, 128], BF16) for h in range(4)]
            RbS = [k.sb(f"RbS{h}", [128, NCH, 128], BF16) for h in range(4)]
            stg2s = [k.sb(f"stg2_{i}", [128, 512], F32) for i in range(4)]
            sgi = [0]

            def nstg():
                sgi[0] += 1
                return stg2s[sgi[0] % 4]
            rint = [k.sb(f"rint{h}", [128, 128], BF16) for h in range(4)]
            rdq = [[k.sb(f"rdq{d}_{h}", [128, 512], BF16) for h in range(4)] for d in range(2)]
            rkd = k.sb("rkd", [128, 8], F32)
            k.dma(rkd[:], V(rkd_in, []))
            for h in range(4):
                stg2 = nstg()
                k.dma(stg2[:, 0:128], V(rint_in[h], []))
                k.cp("act", rint[h][:], stg2[:, 0:128])
                for d in range(2):
                    stg2 = nstg()
                    k.dma(stg2[:], V(rdq_in[d, h], []))
                    k.cp("dve" if d else "act", rdq[d][h][:], stg2[:])
            k.dma(Kall[:], V(KR.rearrange("(h p) t -> p h t", p=128), bKR))
            k.dma(Vall[:], V(VR.rearrange("(n p) c -> p n c", p=128), bVR))
            Rst = [[k.sb(f"Rst{d}_{h}", [128, 128], F32) for h in range(4)] for d in range(2)]
            for d in range(2):
                for h in range(4):
                    k.memset("pool", Rst[d][h][:], 0.0)
            kf = [k.sb(f"kf{i}", [128, 128], BF16) for i in range(8)]
            kfi = 0
            for idx in range(NCH):
                for d in range(2):
                    n = idx if d == 0 else NCH - 1 - idx
                    RS = RfS if d == 0 else RbS
                    for h in range(4):
                        pt = gps()
                        pv = pt.t[:].bitcast(BF16)
                        k.tr(V(pv[:, 0:128], pt.b), Kall[:, h, n * 128:(n + 1) * 128], idb[:])
                        kfc = kf[kfi % 8]
                        kfi += 1
                        k.ts("dve", kfc[:], V(pv[:, 0:128], pt.b), rkd[:, d * 4 + h:d * 4 + h + 1], ALU.mult)
                        pk = gps()
                        k.mm(pk[:, 0:128], kfc[:], Vall[:, n, h * 128:(h + 1) * 128])
                        k.cp("act", RS[h][:, n, :], Rst[d][h][:])
                        k.stt(Rst[d][h][:], Rst[d][h][:], GC[h], pk[:, 0:128], ALU.mult, ALU.add)
                        if (d == 0 and n + 1 == NCH // 2) or (d == 1 and n == NCH // 2):
                            k.ts("dve", Rst[d][h][:], Rst[d][h][:], ppt[:, P_BM:P_BM + 1], ALU.mult)
            Qt = [k.sb(f"Qt{i}", [128, 4, TT_], BF16) for i in range(2)]
            qfb = [[k.sb(f"qfb{d}_{h}", [128, TT_], BF16) for h in range(4)] for d in range(2)]
            sT = [k.sb(f"sT{i}", [128, 128], BF16) for i in range(4)]
            osb = [k.sb(f"osb{i}", [128, 4, 512], F32) for i in range(2)]
            sti = 0
            for ti in range(ntile):
                t0 = ti * TT_
                Q = Qt[ti % 2]
                k.dma(Q[:], V(QR[:, t0:t0 + TT_].rearrange("(h p) t -> p h t", p=128), bQR[ti]))
                for h in range(4):
                    for d in range(2):
                        k.tt("pool" if d == 0 else "dve", qfb[d][h][:], Q[:, h, :], rdq[d][h][:], ALU.mult)
                o_ = osb[ti % 2]
                for s in range(4):
                    n = ti * 4 + s
                    po = gps()
                    for h in range(4):
                        pst = gps()
                        k.mm(pst[:, 0:128], Kall[:, h, n * 128:(n + 1) * 128], Q[:, h, s * 128:(s + 1) * 128])
                        sc_ = sT[sti % 4]
                        sti += 1
                        k.tt("dve", sc_[:], pst[:, 0:128], rint[h][:], ALU.mult)
                        k.mmx(po[:, h * 128:(h + 1) * 128], sc_[:], Vall[:, n, h * 128:(h + 1) * 128], True, False)
                        k.mmx(po[:, h * 128:(h + 1) * 128], qfb[0][h][:, s * 128:(s + 1) * 128], RfS[h][:, n, :], False, False)
                        k.mmx(po[:, h * 128:(h + 1) * 128], qfb[1][h][:, s * 128:(s + 1) * 128], RbS[h][:, n, :], False, True)
                    k.cp("act", o_[:, s, :], po[:])
                k.store(V(OD[t0:t0 + TT_, :].rearrange("(s p) c -> p s c", p=128), bOD[ti]), o_[:])

        if 4 in phases:
            k.new_phase()
            WA = k.sb("WA", [128, 4, D], BF16)
            WB = k.sb("WB", [128, 4, D], BF16)
            WO = k.sb("WO", [128, 8, D], BF16)
            k.dma(WA[:], V(WAs, bW["WAs"]))
            k.dma(WB[:], V(WBs, bW["WBs"]))
            k.dma(WO[:], V(WOs, bW["WOs"]))
            g_bc = k.sb("g_bc", [128, D], F32)
            b_bc = k.sb("b_bc", [128, D], F32)
            k.dma(g_bc[:], V(lnp_in[2:3, :].partition_broadcast(128), []))
            k.dma(b_bc[:], V(lnp_in[3:4, :].partition_broadcast(128), []))
            bufA = k.sb("bufA", [128, 4, 512], F32)
            bufB = k.sb("bufB", [128, 4, 512], F32)
            bufO = k.sb("bufO", [128, 4, 512], F32)
            sq = k.sb("sq", [128, 4, 512], F32)
            ynb = k.sb("ynb", [128, 4, 512], BF16)
            onb = k.sb("onb", [128, 4, 512], BF16)
            s1 = k.sb("s1", [128, 32], F32)
            s2 = k.sb("s2", [128, 32], F32)
            mean = k.sb("mean", [128, 32], F32)
            msq = k.sb("msq", [128, 32], F32)
            var = k.sb("var", [128, 32], F32)
            ggc = [k.sb(f"ggc{i}", [128, 512], F32) for i in range(2)]
            bvgc = [k.sb(f"bvgc{i}", [128, 512], F32) for i in range(2)]
            tmpf = [k.sb(f"tmpf{i}", [128, 512], F32) for i in range(2)]
            SGt = k.sb("SGt", [128, 4, 512], BF16)
            TGt = k.sb("TGt", [128, 16, 512], BF16)
            yaT = [k.sb(f"yaT{i}", [128, 4, 512], BF16) for i in range(2)]
            ybT = [k.sb(f"ybT{i}", [128, 4, 512], BF16) for i in range(2)]
            mT = k.sb("mT", [128, 8, 512], BF16)
            m1 = [k.sb(f"m1_{i}", [128, 512], F32) for i in range(2)]
            m2 = [k.sb(f"m2_{i}", [128, 512], F32) for i in range(2)]
            x1t = [k.sb4(f"x1t{i}", [128, 4, D], F32) for i in range(2)]
            stats = k.sb("stats", [128, 4, 2, 6], F32)
            mv = k.sb("mv", [128, 4, 2], F32)
            rstd = k.sb("rstd", [128, 4], F32)
            nmr = k.sb("nmr", [128, 4], F32)

            st2 = [[k.sb(f"hn{i}_{j}", [128, 32], F32) for j in range(5)] for i in range(2)]

            def head_norm2(specs):
                R = []
                for i, (buf, sqb, nh, hd, eps, outb) in enumerate(specs):
                    v3 = V(buf.t[:].rearrange("p s (h x) -> p (s h) x", x=hd), buf.b)
                    q3 = V(sqb.t[:].rearrange("p s (h x) -> p (s h) x", x=hd), sqb.b)
                    o3 = V(outb.t[:].rearrange("p s (h x) -> p (s h) x", x=hd), outb.b)
                    R.append((buf, sqb, nh, hd, eps, v3, q3, o3) + tuple(st2[i]))
                for (buf, sqb, nh, hd, eps, v3, q3, o3, s1_, s2_, mean_, msq_, var_) in R:
                    k.reduce(s1_[:, 0:nh], v3)
                for (buf, sqb, nh, hd, eps, v3, q3, o3, s1_, s2_, mean_, msq_, var_) in R:
                    k.act(sqb[:], buf[:], AF.Square)
                for (buf, sqb, nh, hd, eps, v3, q3, o3, s1_, s2_, mean_, msq_, var_) in R:
                    k.reduce(s2_[:, 0:nh], q3)
                for (buf, sqb, nh, hd, eps, v3, q3, o3, s1_, s2_, mean_, msq_, var_) in R:
                    k.ts("dve", mean_[:, 0:nh], s1_[:, 0:nh], 1.0 / hd, ALU.mult)
                    k.tt("dve", msq_[:, 0:nh], mean_[:, 0:nh], mean_[:, 0:nh], ALU.mult)
                    k.stt(var_[:, 0:nh], s2_[:, 0:nh], 1.0 / hd, msq_[:, 0:nh], ALU.mult, ALU.subtract)
                    k.ts("dve", var_[:, 0:nh], var_[:, 0:nh], eps, ALU.add)
                for (buf, sqb, nh, hd, eps, v3, q3, o3, s1_, s2_, mean_, msq_, var_) in R:
                    k.act(var_[:, 0:nh], var_[:, 0:nh], AF.Sqrt)
                for (buf, sqb, nh, hd, eps, v3, q3, o3, s1_, s2_, mean_, msq_, var_) in R:
                    k.recip(var_[:, 0:nh], var_[:, 0:nh])
                for (buf, sqb, nh, hd, eps, v3, q3, o3, s1_, s2_, mean_, msq_, var_) in R:
                    mb = V(mean_.t[:, 0:nh].unsqueeze(2).to_broadcast([128, nh, hd]), mean_.b)
                    k.tt("pool", v3, v3, mb, ALU.subtract)
                for bi, (buf, sqb, nh, hd, eps, v3, q3, o3, s1_, s2_, mean_, msq_, var_) in enumerate(R):
                    rb_ = V(var_.t[:, 0:nh].unsqueeze(2).to_broadcast([128, nh, hd]), var_.b)
                    k.tt("pool" if bi == 0 else "dve", o3, v3, rb_, ALU.mult)

            def tview(ap):
                return ap.rearrange("(s p) c -> p s c", p=128)

            def p3_a(ti):
                t0 = ti * TT_
                ya, yb_ = yaT[ti % 2], ybT[ti % 2]
                k.dma(bufA[:], V(tview(YD[0][t0:t0 + TT_, :]), bYD[0][ti]))
                k.dma(bufB[:], V(tview(YD[1][t0:t0 + TT_, :]), bYD[1][ti]))
                k.dma(bufO[:], V(tview(OD[t0:t0 + TT_, :]), bOD[ti]))
                k.dma(SGt[:], V(SG[:, t0:t0 + TT_].rearrange("(h p) t -> p h t", p=128), bSG[ti]))
                k.tt("dve", bufA[:], bufA[:], bufB[:], ALU.add)
                head_norm2([(bufA, bufB, 32, 64, 64e-5, ynb), (bufO, sq, 16, 128, 1e-6, onb)])
                for c in range(4):
                    k.dma(ggc[c % 2][:], V(GG[c * 128:(c + 1) * 128, t0:t0 + TT_], bGG[ti]))
                    k.dma(bvgc[c % 2][:], V(BVG[c * 128:(c + 1) * 128, t0:t0 + TT_], bBVG[ti]))
                    pt = gps()
                    pv = pt.t[:].bitcast(BF16)
                    for s in range(4):
                        k.tr(V(pv[:, s * 128:(s + 1) * 128], pt.b), ynb[:, s, c * 128:(c + 1) * 128], idb[:])
                    k.tt("dve", tmpf[c % 2][:], V(pv[:, 0:512], pt.b), ggc[c % 2][:], ALU.mult)
                    k.tt("pool", ya[:, c, :], tmpf[c % 2][:], bvgc[c % 2][:], ALU.add)
                for h in range(4):
                    pt = gps()
                    pv = pt.t[:].bitcast(BF16)
                    for s in range(4):
                        k.tr(V(pv[:, s * 128:(s + 1) * 128], pt.b), onb[:, s, h * 128:(h + 1) * 128], idb[:])
                    k.stt(yb_[:, h, :], V(pv[:, 0:512], pt.b), 0.5, SGt[:, h, :], ALU.mult, ALU.mult)

            def p3_b(ti):
                t0 = ti * TT_
                xt = x1t[ti % 2]
                ya, yb_ = yaT[ti % 2], ybT[ti % 2]
                k.dma(xt[:], V(tview(X1[t0:t0 + TT_, :]), bX1[ti]))
                k.dma(TGt[:], V(TG[:, t0:t0 + TT_].rearrange("(h p) t -> p h t", p=128), bTG[ti]))
                for dc in range(8):
                    pa = gps()
                    for c in range(4):
                        k.mm(pa[:], WA[:, c, dc * 128:(dc + 1) * 128], ya[:, c, :], c == 0, c == 3)
                    pb = gps()
                    for c in range(4):
                        k.mm(pb[:], WB[:, c, dc * 128:(dc + 1) * 128], yb_[:, c, :], c == 0, c == 3)
                    k.stt(m1[dc % 2][:], TGt[:, dc, :], 1.0, pa[:], ALU.add, ALU.mult)
                    k.stt(m2[dc % 2][:], TGt[:, 8 + dc, :], 1.0, pb[:], ALU.add, ALU.mult)
                    k.tt("pool", mT[:, dc, :], m1[dc % 2][:], m2[dc % 2][:], ALU.add)
                for s in range(4):
                    for hf in range(2):
                        pz = gps()
                        for kk in range(8):
                            k.mm(pz[:], mT[:, kk, s * 128:(s + 1) * 128], WO[:, kk, hf * 512:(hf + 1) * 512], kk == 0, kk == 7)
                        k.stt(xt[:, s, hf * 512:(hf + 1) * 512], xt[:, s, hf * 512:(hf + 1) * 512], 2.0 * ALPHA,
                              pz[:], ALU.mult, ALU.add)
                        bn_stats(stats, s, hf, xt)
                layer_norm_tiles(xt, xt, None, g_bc, b_bc, stats, mv, rstd, nmr, 4.0 * LN_EPS, geng="pool")
                k.store(V(tview(X2[t0:t0 + TT_, :]), bX2[ti]), xt[:])

            p3_a(0)
            for ti in range(1, ntile):
                p3_a(ti)
                p3_b(ti - 1)
            p3_b(ntile - 1)

        def p3_post(ti, xo, xob, xT, next_w, th, sgt, e):
            t0 = ti * TT_
            k.store(V(y_out[t0:t0 + TT_, :].rearrange("(s p) d -> p s d", p=128), []), xo[:])

        if 5 in phases:
            ffn_phase("f2", X2, bX2, W2U, bW["W2U"], W2D, bW["W2D"], 4, p3_post, lambda: None, want_bf=False)

        k.new_phase()
        k.P.finalize(st)
        k.ph.close()
    return nc


def _consts(seq_lens):
    ntok = sum(seq_lens)
    pos = np.concatenate([np.arange(L, dtype=np.float64) for L in seq_lens])
    angle = np.repeat(1.0 / (10000.0 ** np.linspace(0.0, 1.0, 64, dtype=np.float64)), 2)
    theta = pos[:, None] * angle[None, :]
    sgn = np.where(np.arange(128) % 2 == 0, -1.0, 1.0)
    c = {}
    c["cosT"] = np.ascontiguousarray(np.cos(theta).T.astype(np.float32))
    c["sinT"] = np.ascontiguousarray((np.sin(theta) * sgn[None, :]).T.astype(np.float32))
    c["ident"] = np.eye(128, dtype=np.float32)
    rm = np.ones((128, 512), np.float32)
    rm[:, ::128] = 0.0
    c["rmask"] = rm
    bo = np.zeros((128, 128), np.float32)
    bo[:64, :64] = 1.0
    bo[64:, 64:] = 1.0
    c["bones"] = bo
    hs = np.zeros((2, 128, 4, 128), np.float32)
    for cc in range(4):
        for hp in range(2):
            hs[0, hp * 64:(hp + 1) * 64, cc, 32 * cc + hp] = 1.0
            hs[1, 32 * cc + hp, cc, hp * 64:(hp + 1) * 64] = 1.0
    c["hsel"] = hs
    s_ = np.arange(128)[:, None]
    t_ = np.arange(128)[None, :]
    gt, ge, lt, le = (t_ > s_), (t_ >= s_), (t_ < s_), (t_ <= s_)
    c["mask1"] = np.stack([np.concatenate([gt, ge, gt, ge], 1), np.concatenate([lt, le, lt, le], 1)]).astype(np.float32)
    c["maska"] = np.stack([np.tile(lt, (1, 4)), np.tile(gt, (1, 4))]).astype(np.float32)
    idx = np.arange(128, dtype=np.float64)
    sc = 128.0 ** -0.5
    rint = np.zeros((4, 128, 128)); rdq = np.zeros((2, 4, 128, 512)); rkd = np.zeros((128, 8))
    for h in range(4):
        lg = np.log(1.0 - 2.0 ** (-5.0 - h))
        rint[h] = sc * np.exp(lg * np.abs(idx[:, None] - idx[None, :]))
        rdq[0, h] = np.tile(np.exp(lg * (idx + 1.0))[None, :], (128, 4))
        rdq[1, h] = np.tile(np.exp(lg * (128.0 - idx))[None, :], (128, 4))
        rkd[:, h] = sc * np.exp(lg * (127.0 - idx))
        rkd[:, 4 + h] = sc * np.exp(lg * idx)
    c["rint"] = rint.astype(np.float32)
    c["rdq"] = rdq.astype(np.float32)
    c["rkd"] = rkd.astype(np.float32)
    return c


def _weights(inp):
    w = {}
    up = inp["ffn1_up"][0]
    perm = []
    for s in range(NJ // 2):
        perm += list(range(s * 256, (s + 1) * 256)) + list(range(FF + s * 256, FF + (s + 1) * 256))
    perm = np.array(perm)
    w["w1u"] = np.ascontiguousarray(up[:, perm])
    w["w1d"] = np.ascontiguousarray(inp["ffn1_down"][0])
    w["w2u"] = np.ascontiguousarray(inp["ffn2_up"][0][:, perm])
    w["w2d"] = np.ascontiguousarray(inp["ffn2_down"][0])
    w["wa"] = np.ascontiguousarray(inp["w_branch_a"][0])
    w["wb"] = np.ascontiguousarray(inp["w_branch_b"][0])
    w["wo"] = np.ascontiguousarray(inp["w_out"][0])
    w["lnp"] = np.ascontiguousarray(np.stack([inp[n][0] for n in ("ln1_g", "ln1_b", "ln2_g", "ln2_b", "ln3_g", "ln3_b")]))
    win = inp["w_in"][0]
    RW = 1792
    swap = np.arange(128) ^ 1
    cols = list(range(RW))
    q0, k0, v0, g0 = RW, RW + 512, RW + 1024, RW + 1536
    for base in (q0, k0):
        for h in range(4):
            cols += list(range(base + h * 128, base + (h + 1) * 128))
            cols += list(base + h * 128 + swap)
    cols += list(range(g0, g0 + 512))
    cols += list(range(RW + 2048, RW + 2048 + 2048))
    cols += list(range(v0, v0 + 512))
    assert len(cols) == NEXT
    w["win"] = np.ascontiguousarray(win[:, np.array(cols)])
    pp = np.zeros((128, NPP), np.float32)
    def cols(v, n):
        return np.ascontiguousarray(np.asarray(v, np.float32).reshape(n, 128).T)
    pp[:, P_MUP:P_MUP + 14] = cols(inp["shift_prev"][0], 14)
    pp[:, P_MUN:P_MUN + 14] = cols(inp["shift_next"][0], 14)
    pp[:, P_W0:P_W0 + 8] = cols(inp["rwkv_w0"][0], 8)
    pp[:, P_A0:P_A0 + 8] = cols(inp["rwkv_a0"][0], 8)
    pp[:, P_KK:P_KK + 4] = cols(inp["rwkv_k_k"][0], 4)
    pp[:, P_KA:P_KA + 4] = cols(inp["rwkv_k_a"][0], 4)
    pp[:, P_RK:P_RK + 4] = cols(inp["rwkv_r_k"][0], 4)
    pp[:, P_LG:P_LG + 4] = cols(inp["rwkv_lnx_g"][0], 4)
    pp[:, P_LB:P_LB + 4] = cols(inp["rwkv_lnx_b"][0], 4)
    w["pp"] = pp
    w["w2"] = np.ascontiguousarray(np.transpose(inp["rwkv_w2"][0], (1, 0, 2)))
    w["a2"] = np.ascontiguousarray(np.transpose(inp["rwkv_a2"][0], (1, 0, 2)))
    w["g2"] = np.ascontiguousarray(inp["rwkv_g2"][0])
    return w


_NC_CACHE = {}


def kernel(**inputs):
    inp = {k_: np.asarray(v) for k_, v in inputs.items()}
    if "nc" not in _NC_CACHE:
        _NC_CACHE["nc"] = build_program()
    nc = _NC_CACHE["nc"]
    w = _weights(inp)
    c1 = _consts([4096])
    c2 = _consts([2048, 2048])
    xp = np.asarray(inp["x_prompt"], np.float32)
    xs = np.asarray(inp["x_sample"], np.float32)
    in_maps = []
    for core in range(8):
        m = dict(w)
        if core < 4:
            m["x"] = np.ascontiguousarray(xp[core])
            m.update(c1)
            bm = 1.0
        else:
            b0 = 2 * (core - 4)
            m["x"] = np.ascontiguousarray(xs[b0:b0 + 2].reshape(4096, D))
            m.update(c2)
            bm = 0.0
        pp = w["pp"].copy()
        pp[:, P_BM] = bm
        m["pp"] = pp
        in_maps.append(m)
    res = run_bass_kernel_spmd(nc, in_maps, core_ids=list(range(8)))
    outs = [np.asarray(r["y"], np.float32) for r in res.results]
    y_prompt = np.stack(outs[:4]).reshape(4, 4096, D)
    y_sample = np.concatenate([o.reshape(2, 2048, D) for o in outs[4:]], axis=0)
    return (y_prompt, y_sample)
```
